# Optimizing a Trainium2 kernel written in Bass

```python
import jax, jax.numpy as jnp
from jax import lax
import numpy as np

D_MODEL = 2048
BATCH = 2
SEQ = 4096
DEPTH = 1
DEC_BATCH = 8
DEC_SEQ = 4
PAST_LEN = 16384
PAGE_SIZE = 128

D_MIX = D_MODEL
W_A = D_MIX // 2
W_B = D_MIX - W_A
HEAD_DIM_A = 128
N_HEADS_A = W_A // HEAD_DIM_A
HGRN_DK = 128
N_HEADS_B = W_B // HGRN_DK
HGRN_DV = W_B // N_HEADS_B
D_IN = 4 * W_A + 4 * W_B
Q_BLOCK = 128
CHUNK = 64
SB_BIAS_MEAN = -6.0
EPS = 1e-6

kernel_name = "stickbreak_hgrn2_hybrid_step"


def rmsnorm(x, w):
    xf = x.astype(jnp.float32)
    y = xf * lax.rsqrt(jnp.mean(xf * xf, axis=-1, keepdims=True) + EPS)
    return (y * w.astype(jnp.float32)).astype(x.dtype)


def head_rmsnorm(o, gain):
    y = o * lax.rsqrt(jnp.mean(o * o, axis=-1, keepdims=True) + EPS)
    return y * gain.astype(jnp.float32)


def in_proj(x, norm_w, w_in):
    h = rmsnorm(x, norm_w)
    p = jnp.einsum('btd,de->bte', h, w_in)
    offs = [W_A, 2 * W_A, 3 * W_A, 4 * W_A,
            4 * W_A + W_B, 4 * W_A + 2 * W_B, 4 * W_A + 3 * W_B]
    qa, ka, va, za, qb, fb, ib, zb = jnp.split(p, offs, axis=-1)
    B, T = x.shape[:2]
    heads_a = lambda t: t.reshape(B, T, N_HEADS_A, HEAD_DIM_A)
    return (heads_a(qa), heads_a(ka), heads_a(va), za,
            qb.reshape(B, T, N_HEADS_B, HGRN_DK), fb.reshape(B, T, N_HEADS_B, HGRN_DK),
            ib.reshape(B, T, N_HEADS_B, HGRN_DV), zb)


def stick_breaking(q, k, v, bias, q_pos, k_pos):
    scale = HEAD_DIM_A ** -0.5
    z = jnp.einsum('bqhd,bkhd->bhqk', q.astype(jnp.float32), k.astype(jnp.float32)) * scale
    z = z + bias.astype(jnp.float32)[None, :, None, None]
    causal = (k_pos[None, :] < q_pos[:, None])[None, None]
    log_1m = jnp.where(causal, jax.nn.log_sigmoid(-z), 0.0)
    tail = lax.cumsum(log_1m, axis=3, reverse=True) - log_1m
    w = jnp.where(causal, jnp.exp(jax.nn.log_sigmoid(z) + tail), 0.0)
    return jnp.einsum('bhqk,bkhd->bqhd', w, v.astype(jnp.float32))


def stick_breaking_blocks(q, k, v, bias):
    B, T, H, D = q.shape
    nb = T // Q_BLOCK
    qb = q.reshape(B, nb, Q_BLOCK, H, D).transpose(1, 0, 2, 3, 4)
    k_pos = jnp.arange(T)

    def one_block(args):
        qi, i = args
        q_pos = i * Q_BLOCK + jnp.arange(Q_BLOCK)
        return stick_breaking(qi, k, v, bias, q_pos, k_pos)

    o = lax.map(one_block, (qb, jnp.arange(nb)))
    return o.transpose(1, 0, 2, 3, 4).reshape(B, T, H, D)


def hgrn_gates(qb, fb, ib, lb):
    q = jax.nn.silu(qb.astype(jnp.float32))
    g = lb + (1.0 - lb) * jax.nn.sigmoid(fb.astype(jnp.float32))
    return q, 1.0 - g, ib.astype(jnp.float32), jnp.log(g)


def hgrn2_chunked(q, k, v, log_f, s0):
    B, T = q.shape[:2]
    C = min(CHUNK, T)
    pad = (-T) % C
    padw = ((0, 0), (0, pad), (0, 0), (0, 0))
    q, k, v, log_f = (jnp.pad(a, padw) for a in (q, k, v, log_f))
    n = (T + pad) // C
    to_chunks = lambda a: a.reshape(B, n, C, *a.shape[2:]).transpose(1, 0, 2, 3, 4)
    tri = jnp.tril(jnp.ones((C, C), dtype=bool))[None, :, :, None, None]

    def step(S, inp):
        qc, kc, vc, gc = inp
        b = jnp.cumsum(gc, axis=1)
        o_inter = jnp.einsum('bthk,bhkv->bthv', qc * jnp.exp(b), S)
        diff = b[:, :, None] - b[:, None]
        decay = jnp.where(tri, jnp.exp(jnp.minimum(diff, 0.0)), 0.0)
        att = jnp.einsum('btshk,bshk->bhts', qc[:, :, None] * decay, kc)
        o_intra = jnp.einsum('bhts,bshv->bthv', att, vc)
        bl = b[:, -1]
        S_new = S * jnp.exp(bl)[..., None] + jnp.einsum(
            'bshk,bshv->bhkv', kc * jnp.exp(bl[:, None] - b), vc)
        return S_new, o_inter + o_intra

    S_T, o = lax.scan(step, s0, (to_chunks(q), to_chunks(k), to_chunks(v), to_chunks(log_f)))
    o = o.transpose(1, 0, 2, 3, 4).reshape(B, T + pad, N_HEADS_B, HGRN_DV)[:, :T]
    return o, S_T


def out_proj(o_a, o_b, za, zb, gain_a, gain_b, w_out, dtype):
    B, T = o_a.shape[:2]
    a = head_rmsnorm(o_a, gain_a.reshape(N_HEADS_A, HEAD_DIM_A)).reshape(B, T, W_A) \
        * jax.nn.silu(za.astype(jnp.float32))
    b = head_rmsnorm(o_b, gain_b.reshape(N_HEADS_B, HGRN_DV)).reshape(B, T, W_B) \
        * jax.nn.silu(zb.astype(jnp.float32))
    m = jnp.concatenate([a, b], axis=-1).astype(dtype)
    return jnp.einsum('bte,ed->btd', m, w_out).astype(dtype)


def setup_inputs(seed: int = 0) -> dict:
    key = jax.random.key(seed)
    ks = jax.random.split(key, 14)
    n_pages = PAST_LEN // PAGE_SIZE
    n_used = DEC_BATCH * n_pages
    n_phys = n_used + n_used // 4
    nrm = jax.random.normal
    x_prompt = nrm(ks[0], (BATCH, SEQ, D_MODEL), jnp.float32)
    x_sample = nrm(ks[1], (DEC_BATCH, DEC_SEQ, D_MODEL), jnp.float32)
    cache_k = nrm(ks[2], (DEPTH, n_phys, PAGE_SIZE, N_HEADS_A, HEAD_DIM_A), jnp.float32)
    cache_v = nrm(ks[3], (DEPTH, n_phys, PAGE_SIZE, N_HEADS_A, HEAD_DIM_A), jnp.float32)
    state_s = 0.5 * nrm(ks[4], (DEPTH, DEC_BATCH, N_HEADS_B, HGRN_DK, HGRN_DV), jnp.float32)
    page_table = jax.random.permutation(ks[5], n_phys)[:n_used].reshape(
        DEC_BATCH, n_pages).astype(jnp.int32)
    norm_w = 1.0 + 0.02 * nrm(ks[6], (DEPTH, D_MODEL), jnp.float32)
    w_in = nrm(ks[7], (DEPTH, D_MODEL, D_IN), jnp.float32) * D_MODEL ** -0.5
    gain_a = 1.0 + 0.02 * nrm(ks[8], (DEPTH, W_A), jnp.float32)
    gain_b = 1.0 + 0.02 * nrm(ks[9], (DEPTH, W_B), jnp.float32)
    sb_bias = SB_BIAS_MEAN + 0.5 * nrm(ks[13], (DEPTH, N_HEADS_A), jnp.float32)
    lb_logits = 0.5 * nrm(ks[10], (DEPTH + 1, W_B), jnp.float32)
    w_out = nrm(ks[11], (DEPTH, D_MIX, D_MODEL), jnp.float32) * D_MIX ** -0.5
    final_norm_w = 1.0 + 0.02 * nrm(ks[12], (D_MODEL,), jnp.float32)
    return {"x_prompt": x_prompt, "x_sample": x_sample, "cache_k": cache_k,
            "cache_v": cache_v, "state_s": state_s, "page_table": page_table,
            "norm_w": norm_w, "w_in": w_in, "gain_a": gain_a, "gain_b": gain_b,
            "sb_bias": sb_bias, "lb_logits": lb_logits, "w_out": w_out,
            "final_norm_w": final_norm_w}


def reference(x_prompt, x_sample, cache_k, cache_v, state_s, page_table, norm_w, w_in,
              gain_a, gain_b, sb_bias, lb_logits, w_out, final_norm_w):
    n_pages = page_table.shape[1]
    past_len = n_pages * cache_k.shape[2]
    db, ts = x_sample.shape[:2]
    lb_all = jnp.cumsum(jax.nn.softmax(lb_logits.astype(jnp.float32), axis=0), axis=0)
    xp, xs = x_prompt, x_sample
    kp_l, vp_l, sp_l, ks_l, vs_l, ss_l = [], [], [], [], [], []
    for l in range(DEPTH):
        lb = lb_all[l].reshape(N_HEADS_B, HGRN_DK)
        qa, ka, va, za, qb, fb, ib, zb = in_proj(xp, norm_w[l], w_in[l])
        o_a = stick_breaking_blocks(qa, ka, va, sb_bias[l])
        q, k, v, lf = hgrn_gates(qb, fb, ib, lb)
        s0 = jnp.zeros((xp.shape[0], N_HEADS_B, HGRN_DK, HGRN_DV), jnp.float32)
        o_b, s_p = hgrn2_chunked(q, k, v, lf, s0)
        xp = xp + out_proj(o_a, o_b, za, zb, gain_a[l], gain_b[l], w_out[l], xp.dtype)
        kp_l.append(ka)
        vp_l.append(va)
        sp_l.append(s_p.astype(state_s.dtype))
        qa, ka, va, za, qb, fb, ib, zb = in_proj(xs, norm_w[l], w_in[l])
        k_past = cache_k[l][page_table].reshape(db, past_len, N_HEADS_A, HEAD_DIM_A)
        v_past = cache_v[l][page_table].reshape(db, past_len, N_HEADS_A, HEAD_DIM_A)
        k_all = jnp.concatenate([k_past, ka.astype(k_past.dtype)], axis=1)
        v_all = jnp.concatenate([v_past, va.astype(v_past.dtype)], axis=1)
        q_pos = past_len + jnp.arange(ts)
        k_pos = jnp.arange(past_len + ts)
        o_a = stick_breaking(qa, k_all, v_all, sb_bias[l], q_pos, k_pos)
        q, k, v, lf = hgrn_gates(qb, fb, ib, lb)
        o_b, s_s = hgrn2_chunked(q, k, v, lf, state_s[l].astype(jnp.float32))
        xs = xs + out_proj(o_a, o_b, za, zb, gain_a[l], gain_b[l], w_out[l], xs.dtype)
        ks_l.append(ka.astype(cache_k.dtype))
        vs_l.append(va.astype(cache_v.dtype))
        ss_l.append(s_s.astype(state_s.dtype))
    y_prompt = rmsnorm(xp, final_norm_w)
    y_sample = rmsnorm(xs, final_norm_w)
    return (y_prompt, y_sample, jnp.stack(kp_l), jnp.stack(vp_l), jnp.stack(sp_l),
            jnp.stack(ks_l), jnp.stack(vs_l), jnp.stack(ss_l))
```

```python
import numpy as np
import ml_dtypes
import concourse.bass as bass
import concourse.mybir as mybir
from concourse.bass_utils import run_bass_kernel_spmd

F32 = mybir.dt.float32
BF16 = mybir.dt.bfloat16
I32 = mybir.dt.int32
AF = mybir.ActivationFunctionType
ALU = mybir.AluOpType

D = 2048
SEQ = 4096
NTILE = 8
TT = 512
NPH = 1280
NPG = 128
EPS = 1e-6
SCALE = 128 ** -0.5


class Sched:
    def __init__(self, nc):
        self.nc = nc
        self.eng = {}
        self.last_w = {}
        self.readers = {}
        self.dma_sems = {}
        self.n = 0
        self.limit = 10 ** 9
        self.marks = []
        self.log = []

    def mark(self, name):
        self.marks.append((name, self.n))

    def _skip(self):
        self.n += 1
        return self.n > self.limit

    def add_engine(self, name, same_engine_sync=True):
        self.eng[name] = dict(name=name, sem=self.nc.alloc_semaphore("sem_" + name), cnt=0,
                              waited={}, ops=[], same=same_engine_sync)

    def _deps(self, reads, writes):
        evs = []
        for k in reads:
            if k in self.last_w:
                evs.append(self.last_w[k])
            if isinstance(k, tuple) and k[0] == "PS":
                evs += self.readers.get(k, [])
        for k in writes:
            if k in self.last_w:
                evs.append(self.last_w[k])
            evs += self.readers.get(k, [])
        return evs

    def _wait(self, e, evs):
        need = {}
        for (sid, sem, val) in evs:
            if sid == e["name"] and not e["same"]:
                continue
            if e["waited"].get(sid, 0) < val:
                if sid not in need or need[sid][1] < val:
                    need[sid] = (sem, val)
        for sid, (sem, val) in need.items():
            e["waited"][sid] = val
            e["ops"].append(lambda h, sem=sem, val=val: h.wait_ge(sem, val))

    def _record(self, ev, reads, writes):
        for k in reads:
            self.readers.setdefault(k, []).append(ev)
        for k in writes:
            self.last_w[k] = ev
            self.readers[k] = []

    def op(self, engname, fn, reads=(), writes=(), inc=True):
        if self._skip():
            return
        self.log.append((self.n, engname, list(reads), list(writes)))
        e = self.eng[engname]
        self._wait(e, self._deps(reads, writes))
        if inc:
            e["cnt"] += 1
            sem = e["sem"]
            e["ops"].append(lambda h, fn=fn, sem=sem: fn(h).then_inc(sem, 1))
            ev = (e["name"], e["sem"], e["cnt"])
        else:
            e["ops"].append(lambda h, fn=fn: fn(h))
            ev = (e["name"], e["sem"], e["cnt"] + 1)
        self._record(ev, reads, writes)

    def dma(self, qname, fn, reads, writes, semkey):
        if self._skip():
            return
        self.log.append((self.n, "dma:" + qname, list(reads), list(writes)))
        e = self.eng[qname]
        self._wait(e, self._deps(reads, writes))
        if semkey not in self.dma_sems:
            self.dma_sems[semkey] = [self.nc.alloc_semaphore("dsem_%d" % len(self.dma_sems)), 0]
        s = self.dma_sems[semkey]
        s[1] += 16
        sem = s[0]
        e["ops"].append(lambda h, fn=fn, sem=sem: fn(h).then_inc(sem, 16))
        ev = ("dma_%s" % str(semkey), sem, s[1])
        self._record(ev, reads, writes)

    def coll(self, fn, reads, writes, semkey):
        if self._skip():
            return
        e = self.eng["pool"]
        self._wait(e, self._deps(reads, writes))
        if semkey not in self.dma_sems:
            self.dma_sems[semkey] = [self.nc.alloc_semaphore("csem_%d" % len(self.dma_sems)), 0]
        s = self.dma_sems[semkey]
        s[1] += 1
        sem = s[0]
        e["ops"].append(lambda h, fn=fn, sem=sem: fn(h).then_inc(sem))
        ev = ("coll_%s" % str(semkey), sem, s[1])
        self._record(ev, reads, writes)

    def wait_all(self, engname):
        e = self.eng[engname]
        evs = list(self.last_w.values())
        for r in self.readers.values():
            evs += r
        self._wait(e, evs)


def make_consts():
    c = {}
    c["identb"] = np.eye(128, dtype=np.float32).astype(ml_dtypes.bfloat16)
    j = np.arange(128)[:, None]
    s = np.arange(128)[None, :]
    c["negtri"] = (-(j >= s).astype(np.float32)).astype(ml_dtypes.bfloat16)
    c["negcomp"] = (-(j < s).astype(np.float32)).astype(ml_dtypes.bfloat16)
    c["negones"] = (-np.ones((128, 128), np.float32)).astype(ml_dtypes.bfloat16)
    c["onesm"] = (np.ones((128, 128), np.float32) / 128.0).astype(ml_dtypes.bfloat16)
    c["dmask"] = (j < s).astype(np.float32)
    c["imask"] = (j <= s).astype(np.float32)
    c["iota"] = np.arange(128, dtype=np.float32).reshape(128, 1)
    c["identf"] = np.eye(128, dtype=np.float32)
    c["onesf"] = np.ones((128, 128), np.float32)

    def hg_mats(T):
        mid = T // 2 - 1
        sp = np.arange(T)[:, None]
        ss = np.arange(T)[None, :]
        mtok = np.zeros((T, T), np.float32)
        mtok[(ss < sp) & (sp <= mid)] = 1.0
        mtok[(sp >= mid + 1) & (sp <= ss)] = -1.0
        mfeat = np.zeros((T, T + 2), np.float32)
        mfeat[:, :T] = -mtok
        mfeat[: mid + 1, T] = 1.0
        mfeat[mid + 1:, T + 1] = 1.0
        return mtok, mfeat
    c["mtok"], c["mfeat"] = hg_mats(128)
    c["mtok4"], c["mfeat4"] = hg_mats(4)
    return c


CONST_SPECS = [("identb", [128, 128], BF16), ("negtri", [128, 128], BF16), ("negcomp", [128, 128], BF16),
               ("negones", [128, 128], BF16), ("onesm", [128, 128], BF16), ("dmask", [128, 128], F32),
               ("imask", [128, 128], F32), ("iota", [128, 1], F32), ("mtok", [128, 128], F32),
               ("mfeat", [128, 130], F32), ("mtok4", [4, 4], F32), ("mfeat4", [4, 6], F32),
               ("identf", [128, 128], F32), ("onesf", [128, 128], F32)]

IN_SPECS = [("xfull", [-1, D], F32), ("xres", [1024, D], F32), ("xs4", [16, D], F32), ("xsres", [4, D], F32),
            ("win", [D, D], F32), ("wout", [D, D], F32), ("normw", [128, 16], F32), ("gains", [128, 4], F32),
            ("sbb", [128, 2], F32), ("lbl", [128, 2, 256], F32), ("fnw", [128, D], F32),
            ("ck", [None, 256], F32), ("cv", [None, 256], F32), ("pt", [128, 4, NPG], I32),
            ("st", [4, 2, 128, 128], F32), ("rank", [1, 1], I32), ("oidx", [128, 16], I32)]

OUT_SPECS = [("yp", [1024, D], F32), ("ys", [4, D], F32), ("kn", [SEQ, 256], F32), ("vn", [SEQ, 256], F32),
             ("spn", [2, 128, 128], F32), ("ksn", [16, 256], F32), ("vsn", [16, 256], F32),
             ("ssn", [4, 2, 128, 128], F32)]


def build_program(nph=NPH, ntiles_run=NTILE, do_sample=True, do_outproj=True, ndb=4, limit=None, xrows=SEQ):
    nc = bass.Bass("TRN2", target_bir_lowering=False)
    I = {}
    for name, shp, dt in IN_SPECS + CONST_SPECS:
        shp = [nph * 128 if d is None else (xrows if d == -1 else d) for d in shp]
        I[name] = nc.dram_tensor(name, shp, dt, kind="ExternalInput").ap()
    O = {}
    for name, shp, dt in OUT_SPECS:
        O[name] = nc.dram_tensor(name, shp, dt, kind="ExternalOutput").ap()
    SRC = [nc.dram_tensor("src%d" % i, [2048, 128], BF16) for i in range(NTILE)]
    GAT = [nc.dram_tensor("gat%d" % i, [8192, 128], BF16) for i in range(NTILE)]
    SRCS = nc.dram_tensor("srcs", [2048, 16], BF16)
    GATS = nc.dram_tensor("gats", [8192, 16], BF16)

    S = Sched(nc)
    if limit is not None:
        S.limit = limit
    nc._sched = S
    S.add_engine("pe", same_engine_sync=False)
    S.add_engine("act")
    S.add_engine("dve")
    S.add_engine("pool")
    S.add_engine("sp")

    def sb(name, shape, dt):
        return nc.alloc_sbuf_tensor("sb_" + name, shape, dt)

    C = {}
    for name, shp, dt in CONST_SPECS:
        C[name] = sb("c_" + name, shp, dt)
    WC = sb("WC", [128, 16, D], BF16)
    HT = sb("HT", [128, 16, TT], BF16)
    XT = [sb("XT%d" % i, [128, D], F32) for i in range(2)]
    XB = sb("XB", [128, D], BF16)
    normw = sb("normw", [128, 16], F32)
    gains = sb("gains", [128, 4], F32)
    sbb = sb("sbb", [128, 2], F32)
    lbl = sb("lbl", [128, 2, 256], F32)
    omlb = sb("omlb", [128, 256], F32)
    st1 = sb("st1", [128, 4], F32)
    QT = sb("QT", [128, 2, TT], BF16)
    ZS = sb("ZS", [128, 4, TT], F32)
    QB = sb("QB", [128, 2, TT], F32)
    KVS = [sb("KVS%d" % i, [128, 512], F32) for i in range(2)]
    KB = sb("KB", [128, 256], BF16)
    SN = sb("SN", [128, 256], F32)
    KK = sb("KK", [128, 256], F32)
    LG = sb("LG", [128, 256], F32)
    VB = sb("VB", [128, 256], BF16)
    EK = sb("EK", [128, 128], F32)
    KH = sb("KH", [128, 128], BF16)
    KHT = sb("KHT", [128, 128], BF16)
    EQ = sb("EQ", [128, 128], F32)
    QH = sb("QH", [128, 128], BF16)
    ECL = sb("ECL", [128, 4], F32)
    ATT = sb("ATT", [128, 128], BF16)
    SST = [sb("SST%d" % i, [128, 128], F32) for i in range(2)]
    SCB = sb("SCB", [128, 128], BF16)
    SL = sb("SL", [128, 128], F32)
    EE = [sb("EE%d" % i, [128, TT], F32) for i in range(2)]
    SPB = [sb("SPB%d" % i, [128, TT], BF16) for i in range(2)]
    XX = [sb("XX%d" % i, [128, TT], F32) for i in range(2)]
    AA = [sb("AA%d" % i, [128, TT], BF16) for i in range(2)]
    OSQ = sb("OSQ", [128, TT], BF16)
    OSB = sb("OSB", [128, TT], F32)
    RIN = sb("RIN", [128, TT], F32)
    MT = sb("MT", [128, TT], BF16)
    SC = sb("SC", [128, 4], F32)
    ENW = sb("ENW", [4, 8], F32)
    SPN = sb("SPN", [4, 4], BF16)
    ANW = sb("ANW", [4, 8], BF16)
    OSM = sb("OSM", [4, 256], F32)
    PJS = sb("PJS", [4, 512], F32)
    KNT = sb("KNT", [128, 2, 4], BF16)
    VNB = sb("VNB", [4, 256], BF16)
    RK = sb("RK", [1, 1], I32)
    NPB = 8

    from contextlib import ExitStack
    pstack = ExitStack()
    KT = pstack.enter_context(nc.sbuf_tensor("sb_KT", [128, 2, SEQ], BF16))
    VV = pstack.enter_context(nc.sbuf_tensor("sb_VV", [128, 32, 256], BF16))
    PS = [nc.alloc_psum_tensor("ps%d" % i, [128, 512], F32) for i in range(8)]
    PSB = [p[:].bitcast(BF16) for p in PS]

    def pk(b):
        return ("PS", b)

    def dma(q, out, in_, reads, writes, key, **kw):
        S.dma(q, lambda h: h.dma_start(out=out, in_=in_, **kw), reads, writes, key)

    def mm(out, lhsT, rhs, start, stop, reads, writes, inc=None):
        if inc is None:
            inc = stop
        S.op("pe", lambda h: h.matmul(out, lhsT, rhs, start=start, stop=stop), reads, writes, inc=inc)

    def tr(out, in_, ident, reads, writes, inc=True):
        S.op("pe", lambda h: h.transpose(out, in_, ident), reads, writes, inc=inc)

    def act(out, in_, func, reads, writes, **kw):
        S.op("act", lambda h: h.activation(out=out, in_=in_, func=func, **kw), reads, writes)

    def tt(eng, out, in0, in1, op, reads, writes):
        S.op(eng, lambda h: h.tensor_tensor(out=out, in0=in0, in1=in1, op=op), reads, writes)

    def ts(eng, out, in0, s1, s2, op0, op1, reads, writes):
        if op1 is None:
            S.op(eng, lambda h: h.tensor_scalar(out=out, in0=in0, scalar1=s1, scalar2=None, op0=op0), reads, writes)
        else:
            S.op(eng, lambda h: h.tensor_scalar(out=out, in0=in0, scalar1=s1, scalar2=s2, op0=op0, op1=op1),
                 reads, writes)

    def stt(out, in0, sc, in1, op0, op1, reads, writes):
        S.op("dve", lambda h: h.scalar_tensor_tensor(out=out, in0=in0, scalar=sc, in1=in1, op0=op0, op1=op1),
             reads, writes)

    def cp(eng, out, in_, reads, writes):
        if eng == "act_copy":
            act(out, in_, AF.Copy, reads, writes)
        else:
            S.op(eng, lambda h: h.tensor_copy(out=out, in_=in_), reads, writes)

    def rec(out, in_, reads, writes):
        S.op("dve", lambda h: h.reciprocal(out=out, in_=in_), reads, writes)

    for name, shp, dt in CONST_SPECS:
        dma("sp", C[name][:], I[name], [], [("c", name)], ("c", name))
    ck_all = [("c", n) for n, _, _ in CONST_SPECS]
    dma("sp", normw[:], I["normw"], [], ["normw"], "normw")
    dma("sp", gains[:], I["gains"], [], ["gains"], "gains")
    dma("sp", sbb[:], I["sbb"], [], ["sbb"], "sbb")
    dma("sp", lbl[:], I["lbl"], [], ["lbl"], "lbl")
    dma("sp", RK[:], I["rank"], [], ["RK"], "RK")
    tt("dve", omlb[:], lbl[:, 1, :], lbl[:, 0, :], ALU.subtract, ["lbl"], ["omlb"])
    act(omlb[:], omlb[:], AF.Sigmoid, ["omlb"], ["omlb"])

    for ch in range(16):
        xt = XT[ch % 2]
        k = ("XT", ch % 2)
        dma("sp", xt[:], I["win"][ch * 128:(ch + 1) * 128, :], [], [k], k)
        eng = "dve" if ch % 2 == 0 else "pool"
        ts(eng, WC[:, ch, :], xt[:], normw[:, ch:ch + 1], None, ALU.mult, None, [k, "normw"], [("WC", ch)])
    WCK = [("WC", ch) for ch in range(16)]

    def load_norm_block(src_ap, nt, col0, par):
        xt = XT[par]
        k = ("XT", par)
        dma("sp", xt[0:nt, :], src_ap, [], [k], k)
        S.op("act", lambda h: h.activation(out=XB[0:nt, :], in_=xt[0:nt, :], func=AF.Square,
                                           accum_out=st1[0:nt, 0:1]), [k], ["XB", "st1"])
        act(st1[0:nt, 1:2], st1[0:nt, 0:1], AF.Sqrt, ["st1"], ["st1"], scale=1.0 / D, bias=EPS)
        rec(st1[0:nt, 2:3], st1[0:nt, 1:2], ["st1"], ["st1"])
        ts("dve", XB[0:nt, 0:1024], xt[0:nt, 0:1024], st1[0:nt, 2:3], None, ALU.mult, None, [k, "st1"], ["XB"])
        ts("pool", XB[0:nt, 1024:2048], xt[0:nt, 1024:2048], st1[0:nt, 2:3], None, ALU.mult, None,
           [k, "st1"], ["XBb"])
        for g in range(4):
            b = 6 + (g % 2)
            for c4 in range(4):
                ch = g * 4 + c4
                tr(PSB[b][:, c4 * 128:c4 * 128 + nt], XB[0:nt, ch * 128:(ch + 1) * 128], C["identb"][0:nt, 0:nt],
                   ["XB", "XBb", ("c", "identb")], [pk(b)], inc=(c4 == 3))
            src = PSB[b][:, 0:512].rearrange("p (c t) -> p c t", c=4)[:, :, 0:nt]
            cp("dve" if g % 2 == 0 else "act_copy", HT[:, g * 4:(g + 1) * 4, col0:col0 + nt], src, [pk(b)],
               [("HT", g)])

    HTK = [("HT", g) for g in range(4)]

    def inproj_feature(nt):
        for fb in range(8):
            b = fb % 2
            for ch in range(16):
                mm(PS[b][:, 0:nt], WC[:, ch, fb * 128:(fb + 1) * 128], HT[:, ch, 0:nt], ch == 0, ch == 15,
                   WCK + HTK, [pk(b)])
            if fb < 2:
                cp("dve", QT[:, fb, 0:nt], PS[b][:, 0:nt], [pk(b)], [("QT", fb)])
            elif fb < 4:
                act(ZS[:, fb - 2, 0:nt], PS[b][:, 0:nt], AF.Silu, [pk(b)], [("ZS", fb - 2)])
            elif fb < 6:
                act(QB[:, fb - 4, 0:nt], PS[b][:, 0:nt], AF.Silu, [pk(b)], [("QB", fb - 4)])
            else:
                act(ZS[:, fb - 4, 0:nt], PS[b][:, 0:nt], AF.Silu, [pk(b)], [("ZS", fb - 4)])

    def inproj_token(nt, c0, grp, b):
        for ch in range(16):
            mm(PS[b][0:nt, :], HT[:, ch, c0:c0 + nt], WC[:, ch, 1024 + grp * 512:1024 + (grp + 1) * 512],
               ch == 0, ch == 15, WCK + HTK, [pk(b)])

    def headnorm_store(ob, slot, nt, dst_fn):
        act(OSQ[:, 0:nt], PS[ob][:, 0:nt], AF.Square, [pk(ob)], ["OSQ"])
        cp("dve", OSB[:, 0:nt], PS[ob][:, 0:nt], [pk(ob)], ["OSB"])
        mm(PS[0][:, 0:nt], C["onesm"][:], OSQ[:, 0:nt], True, True, ["OSQ", ("c", "onesm")], [pk(0)])
        act(RIN[:, 0:nt], PS[0][:, 0:nt], AF.Sqrt, [pk(0)], ["RIN"], bias=EPS)
        rec(RIN[:, 0:nt], RIN[:, 0:nt], ["RIN"], ["RIN"])
        tt("dve", OSB[:, 0:nt], OSB[:, 0:nt], RIN[:, 0:nt], ALU.mult, ["OSB", "RIN"], ["OSB"])
        stt(MT[:, 0:nt], ZS[:, slot, 0:nt], gains[:, slot:slot + 1], OSB[:, 0:nt], ALU.mult, ALU.mult,
            [("ZS", slot), "gains", "OSB"], ["MT"])
        dst_fn(slot)

    def hgrn_chunk(T, hb, qcol, ob, oc, mtok, mfeat, mk, start_group):
        hs = slice(hb * 128, (hb + 1) * 128)
        mm(PS[2][0:T, 0:128], mtok, LG[0:T, hs], True, True, ["LG"] + mk, [pk(2)])
        mm(PS[3][:, 0:T + 2], LG[0:T, hs], mfeat, True, True, ["LG"] + mk, [pk(3)])
        act(EK[0:T, :], PS[2][0:T, 0:128], AF.Exp, [pk(2)], ["EK"])
        tt("dve", KH[0:T, :], KK[0:T, hs], EK[0:T, :], ALU.mult, ["KK", "EK"], ["KH"])
        act(EQ[:, 0:T], PS[3][:, 0:T], AF.Exp, [pk(3)], ["EQ"])
        tt("dve", QH[:, 0:T], QB[:, hb, qcol:qcol + T], EQ[:, 0:T], ALU.mult, [("QB", hb), "EQ"], ["QH"])
        act(ECL[:, 0:2], PS[3][:, T:T + 2], AF.Exp, [pk(3)], ["ECL"])
        tt("dve", ECL[:, 2:3], ECL[:, 0:1], ECL[:, 1:2], ALU.mult, ["ECL"], ["ECL"])
        tr(PSB[4][:, 0:T], KH[0:T, :], C["identb"][0:T, 0:T], ["KH", ("c", "identb")], [pk(4)])
        cp("dve", KHT[:, 0:T], PSB[4][:, 0:T], [pk(4)], ["KHT"])
        mm(PS[5][0:T, 0:T], KHT[:, 0:T], QH[:, 0:T], True, True, ["KHT", "QH"], [pk(5)])
        ts("dve", EK[0:T, 0:T], PS[5][0:T, 0:T], 1e30, -1e30, ALU.min, ALU.max, [pk(5), "EK"], ["EK"])
        tt("dve", ATT[0:T, 0:T], EK[0:T, 0:T], C["imask"][0:T, 0:T], ALU.mult, ["EK", ("c", "imask")], ["ATT"])
        sk = ("SST", hb)
        ts("dve", SCB[:], SST[hb][:], ECL[:, 0:1], None, ALU.mult, None, [sk, "ECL"], ["SCB"])
        ts("pool", SL[:], SST[hb][:], ECL[:, 2:3], None, ALU.mult, None, [sk, "ECL"], ["SL"])
        mm(PS[ob][:, oc:oc + T], SCB[:], QH[:, 0:T], True, False, ["SCB", "QH"], [pk(ob)], inc=False)
        mm(PS[ob][:, oc:oc + T], VB[0:T, hs], ATT[0:T, 0:T], False, True, ["VB", "ATT"], [pk(ob)])
        mm(PS[6][:, 0:128], KH[0:T, :], VB[0:T, hs], True, True, ["KH", "VB"], [pk(6)])
        stt(SST[hb][:], PS[6][:, 0:128], ECL[:, 1:2], SL[:], ALU.mult, ALU.add, [pk(6), "ECL", "SL"], [sk])

    def hgrn_gates(T, b):
        act(SN[0:T, :], PS[b][0:T, 0:256], AF.Sigmoid, [pk(b)], ["SN"], scale=-1.0)
        cp("dve", VB[0:T, :], PS[b][0:T, 256:512], [pk(b)], ["VB"])
        tt("dve", KK[0:T, :], SN[0:T, :], omlb[0:T, :], ALU.mult, ["SN", "omlb"], ["KK"])
        act(LG[0:T, :], KK[0:T, :], AF.Ln, ["KK"], ["LG"], scale=-1.0, bias=1.0)

    mk128 = [("c", "mtok"), ("c", "mfeat")]
    mk4 = [("c", "mtok4"), ("c", "mfeat4")]

    for hb in range(2):
        S.op("pool", lambda h, hb=hb: h.memset(SST[hb][:], 0.0), [], [("SST", hb)])

    for it in range(ntiles_run):
        t0 = it * TT
        S.mark('tile%d_x' % it)
        for blk in range(4):
            load_norm_block(I["xfull"][t0 + blk * 128:t0 + (blk + 1) * 128, :], 128, blk * 128, blk % 2)
        S.mark('inproj_feat')
        inproj_feature(TT)
        S.mark('groupA')
        for blk in range(4):
            gb = it * 4 + blk
            b = blk % 2
            inproj_token(128, blk * 128, 0, b)
            kv = KVS[blk % 2]
            kk_ = ("KVS", blk % 2)
            cp("act_copy", kv[:], PS[b][:, :], [pk(b)], [kk_])
            dma("sp", O["kn"][t0 + blk * 128:t0 + (blk + 1) * 128, :], kv[:, 0:256], [kk_], [], ("okn", blk % 2))
            dma("sp", O["vn"][t0 + blk * 128:t0 + (blk + 1) * 128, :], kv[:, 256:512], [kk_], [], ("ovn", blk % 2))
            cp("pool", VV[:, gb, :], kv[:, 256:512], [kk_], [("VV", gb)])
            cp("pool", KB[:], kv[:, 0:256], [kk_], ["KB"])
            for h in range(2):
                tr(PSB[7][:, h * 128:(h + 1) * 128], KB[:, h * 128:(h + 1) * 128], C["identb"][:],
                   ["KB", ("c", "identb")], [pk(7)], inc=(h == 1))
            cp("dve", KT[:, :, gb * 128:(gb + 1) * 128], PSB[7][:, 0:256].rearrange("p (h t) -> p h t", h=2),
               [pk(7)], [("KT", gb)])
        S.mark('hgrn')
        for blk in range(4):
            b = 0
            inproj_token(128, blk * 128, 1, b)
            hgrn_gates(128, b)
            for hb in range(2):
                hgrn_chunk(128, hb, blk * 128, 7 if hb == 0 else 1, blk * 128, C["mtok"][:], C["mfeat"][:], mk128,
                           blk == 0)
        S.mark('headnormB')
        def dst_prompt(slot, it=it):
            dma("sp", SRC[it].ap().rearrange("(j f) t -> f j t", j=4)[slot * 128:(slot + 1) * 128, :, :],
                MT[:].rearrange("p (j t) -> p j t", j=4), ["MT"], [("SRC", it)], ("src", it))
        headnorm_store(7, 2, TT, dst_prompt)
        headnorm_store(1, 3, TT, dst_prompt)
        S.mark('attn')
        for h in range(2):
            nkb = 4 * it + 4
            first = True
            for kb in range(nkb - 1, -1, -1):
                j = kb - 4 * it
                qs = 128 * j if j > 0 else 0
                par = kb % 2
                zb = 2 + par
                qsl = slice(qs, TT)
                mm(PS[zb][:, qsl], KT[:, h, kb * 128:(kb + 1) * 128], QT[:, h, qsl], True, True,
                   [("KT", kb), ("QT", h)], [pk(zb)])
                ek = ("EE", par)
                act(EE[par][:, qsl], PS[zb][:, qsl], AF.Exp, [pk(zb), "sbb"], [ek], scale=SCALE,
                    bias=sbb[:, h:h + 1])
                if j >= 0:
                    tt("dve", EE[par][:, qs:qs + 128], EE[par][:, qs:qs + 128], C["dmask"][:], ALU.mult,
                       [ek, ("c", "dmask")], [ek])
                act(SPB[par][:, qsl], EE[par][:, qsl], AF.Ln, [ek], [("SPB", par)], bias=1.0)
                mm(PS[4][:, qsl], C["negtri"][:], SPB[par][:, qsl], first, True,
                   [("SPB", par), ("c", "negtri")], [pk(4)])
                act(XX[par][:, qsl], PS[4][:, qsl], AF.Exp, [pk(4)], [("XX", par)])
                if kb > 0:
                    mm(PS[4][:, qsl], C["negcomp"][:], SPB[par][:, qsl], False, True,
                       [("SPB", par), ("c", "negcomp"), ("XX", par)], [pk(4)])
                tt("dve", AA[par][:, qsl], EE[par][:, qsl], XX[par][:, qsl], ALU.mult, [ek, ("XX", par)],
                   [("AA", par)])
                mm(PS[5][:, qsl], VV[:, kb, h * 128:(h + 1) * 128], AA[par][:, qsl], first, True,
                   [("VV", kb), ("AA", par)], [pk(5)])
                first = False
            headnorm_store(5, h, TT, dst_prompt)
        S.mark('allgather')
        S.coll(lambda g, it=it: g.collective_compute("AllGather", ALU.bypass,
                                                     replica_groups=[[0, 1, 2, 3], [4, 5, 6, 7]],
                                                     ins=[SRC[it].ap().opt()], outs=[GAT[it].ap().opt()]),
               [("SRC", it)], [("GAT", it)], ("cc", it))

    for hb in range(2):
        dma("sp", O["spn"][hb], SST[hb][:], [("SST", hb)], [], ("ospn", hb))

    def barrier():
        for en in ("pe", "act", "dve", "pool", "sp"):
            S.wait_all(en)
    barrier()
    pstack.close()
    sstack = ExitStack()
    PTS = sstack.enter_context(nc.sbuf_tensor("sb_PTS", [128, 4, NPG], I32))
    IDX = sstack.enter_context(nc.sbuf_tensor("sb_IDX", [128, 4, NPG], I32))
    PST = [sstack.enter_context(nc.sbuf_tensor("sb_PST%d" % i, [128, NPB, 256], F32)) for i in range(2)]
    VSG = [sstack.enter_context(nc.sbuf_tensor("sb_VSG%d" % i, [128, NPB, 256], BF16)) for i in range(2)]
    KTP = [sstack.enter_context(nc.sbuf_tensor("sb_KTP%d" % i, [128, 2, NPB * 128], BF16)) for i in range(2)]

    dma("sp", PTS[:], I["pt"], [], ["PTS"], "PTS")
    S.op("dve", lambda h: h.tensor_scalar(out=IDX[:], in0=PTS[:], scalar1=128.0, scalar2=C["iota"][:, 0:1],
                                          op0=ALU.mult, op1=ALU.add), ["PTS", ("c", "iota")], ["IDX"])
    ngrp = NPG // NPB
    for db in range(ndb if do_sample else 0):
        load_norm_block(I["xs4"][db * 4:(db + 1) * 4, :], 4, 0, db % 2)
        inproj_feature(4)
        inproj_token(4, 0, 0, 0)
        cp("act_copy", PJS[0:4, 0:512], PS[0][0:4, :], [pk(0)], ["PJS"])
        dma("sp", O["ksn"][db * 4:(db + 1) * 4, :], PJS[0:4, 0:256], ["PJS"], [], "oksn")
        dma("sp", O["vsn"][db * 4:(db + 1) * 4, :], PJS[0:4, 256:512], ["PJS"], [], "ovsn")
        cp("dve", VNB[0:4, :], PJS[0:4, 256:512], ["PJS"], ["VNB"])
        cp("dve", KB[0:4, :], PJS[0:4, 0:256], ["PJS"], ["KB"])
        for h in range(2):
            tr(PSB[7][:, h * 4:(h + 1) * 4], KB[0:4, h * 128:(h + 1) * 128], C["identb"][0:4, 0:4],
               ["KB", ("c", "identb")], [pk(7)], inc=(h == 1))
        cp("dve", KNT[:, :, :], PSB[7][:, 0:8].rearrange("p (h t) -> p h t", h=2), [pk(7)], ["KNT"])
        for hb in range(2):
            dma("sp", SST[hb][:], I["st"][db, hb], [], [("SST", hb)], ("sst", hb))
        inproj_token(4, 0, 1, 0)
        hgrn_gates(4, 0)
        for hb in range(2):
            hgrn_chunk(4, hb, 0, 7 if hb == 0 else 1, 0, C["mtok4"][:], C["mfeat4"][:], mk4, True)
            dma("sp", O["ssn"][db, hb], SST[hb][:], [("SST", hb)], [], ("ossn", hb))

        def dst_sample(slot, db=db):
            dma("sp", SRCS.ap().rearrange("(j f) t -> f j t", j=4)[slot * 128:(slot + 1) * 128, db, 0:4],
                MT[:, 0:4], ["MT"], [("SRCS",)], "srcs")
        headnorm_store(7, 2, 4, dst_sample)
        headnorm_store(1, 3, 4, dst_sample)
        for g in range(ngrp):
            par = g % 2
            for pp in range(NPB):
                p = g * NPB + pp
                S.dma("pool", lambda h_, par=par, pp=pp, p=p, db=db: h_.indirect_dma_start(
                    out=PST[par][:, pp, :], out_offset=None, in_=I["ck"],
                    in_offset=bass.IndirectOffsetOnAxis(ap=IDX[:, db, p:p + 1], axis=0)),
                    ["IDX"], [("PST", par)], ("pst", par))
            for q4 in range(NPB // 4):
                for h in range(2):
                    for i4 in range(4):
                        pp = q4 * 4 + i4
                        S.op("pe", lambda h_, par=par, pp=pp, h=h, i4=i4: h_.transpose(
                            PS[6 + h][:, i4 * 128:(i4 + 1) * 128], PST[par][:, pp, h * 128:(h + 1) * 128],
                            C["identf"][:]), [("PST", par), ("c", "identf")], [pk(6 + h)], inc=(i4 == 3))
                    cp("dve" if h == 0 else "act_copy", KTP[par][:, h, q4 * 512:(q4 + 1) * 512], PS[6 + h][:, :],
                       [pk(6 + h)], [("KTP", par, h, q4)])
            for pp in range(NPB):
                p = g * NPB + pp
                for h in range(2):
                    mm(PS[2 + h][:, p * 4:p * 4 + 4], KTP[par][:, h, pp * 128:(pp + 1) * 128], QT[:, h, 0:4],
                       True, True, [("KTP", par, h, pp // 4), ("QT", h)], [pk(2 + h)],
                       inc=(pp == NPB - 1 and h == 1))
        for h in range(2):
            zb = 2 + h
            act(EE[h][:, :], PS[zb][:, :], AF.Exp, [pk(zb), "sbb"], [("EE", h)], scale=SCALE, bias=sbb[:, h:h + 1])
            act(SPB[h][:, :], EE[h][:, :], AF.Ln, [("EE", h)], [("SPB", h)], bias=1.0)
            mm(PS[0][:, :], C["negones"][:], SPB[h][:, :], True, True, [("SPB", h), ("c", "negones")], [pk(0)])
            cp("dve", XX[0][:, :], PS[0][:, :], [pk(0)], [("XX", 0)])
            x0v = XX[0][:].rearrange("p (a q) -> p q a", q=4)
            x1v = XX[1][:].rearrange("p (a q) -> p q a", q=4)
            for q in range(4):
                S.op("dve", lambda h_, q=q, x0v=x0v, x1v=x1v: h_.tensor_tensor_scan(
                    out=x1v[:, q, :], data0=C["onesf"][:, 0:NPG], data1=x0v[:, q, :], initial=0.0,
                    op0=ALU.mult, op1=ALU.add), [("XX", 0), ("c", "onesf")], [("XX", 1)])
            mm(PS[1][0:4, 0:4], KNT[:, h, :], QT[:, h, 0:4], True, True, ["KNT", ("QT", h)], [pk(1)])
            act(ENW[0:4, 0:4], PS[1][0:4, 0:4], AF.Exp, [pk(1), "sbb"], ["ENW"], scale=SCALE,
                bias=sbb[0:4, h:h + 1])
            tt("dve", ENW[0:4, 0:4], ENW[0:4, 0:4], C["dmask"][0:4, 0:4], ALU.mult, ["ENW", ("c", "dmask")], ["ENW"])
            act(SPN[0:4, 0:4], ENW[0:4, 0:4], AF.Ln, ["ENW"], ["SPN"], bias=1.0)
            mm(PS[1][:, 4:8], C["negones"][0:4, :], SPN[0:4, 0:4], True, True, ["SPN", ("c", "negones")], [pk(1)])
            tt("dve", SC[:, 0:4], XX[1][:, 508:512], PS[1][:, 4:8], ALU.add, [("XX", 1), pk(1)], ["SC"])
            for q in range(4):
                ts("dve", x0v[:, q, :], x1v[:, q, :], -1.0, SC[:, q:q + 1], ALU.mult, ALU.add,
                   [("XX", 1), "SC"], [("XX", 0)])
            mm(PS[4][:, :], C["negtri"][:], SPB[h][:, :], True, True, [("SPB", h), ("c", "negtri")], [pk(4)])
            tt("dve", XX[1][:, :], PS[4][:, :], XX[0][:, :], ALU.add, [pk(4), ("XX", 0)], [("XX", 1)])
            act(XX[1][:, :], XX[1][:, :], AF.Exp, [("XX", 1)], [("XX", 1)])
            tt("dve", AA[h][:, :], EE[h][:, :], XX[1][:, :], ALU.mult, [("EE", h), ("XX", 1)], [("AA", h)])
            mm(PS[1][0:4, 8:12], C["negtri"][0:4, 0:4], SPN[0:4, 0:4], True, True, ["SPN", ("c", "negtri")], [pk(1)])
            act(ENW[0:4, 4:8], PS[1][0:4, 8:12], AF.Exp, [pk(1)], ["ENW"])
            tt("dve", ANW[0:4, h * 4:h * 4 + 4], ENW[0:4, 0:4], ENW[0:4, 4:8], ALU.mult, ["ENW"], ["ANW"])
        obk = [5, 1]
        for g in range(ngrp):
            par = g % 2
            for pp in range(NPB):
                p = g * NPB + pp
                S.dma("pool", lambda h_, par=par, pp=pp, p=p, db=db: h_.indirect_dma_start(
                    out=PST[par][:, pp, :], out_offset=None, in_=I["cv"],
                    in_offset=bass.IndirectOffsetOnAxis(ap=IDX[:, db, p:p + 1], axis=0)),
                    ["IDX"], [("PST", par)], ("pst", par))
            cp("dve" if g % 2 == 0 else "act_copy", VSG[par][:, :, :], PST[par][:, :, :], [("PST", par)],
               [("VSG", par)])
            for pp in range(NPB):
                p = g * NPB + pp
                for h in range(2):
                    mm(PS[obk[h]][0:4, 0:128], AA[h][:, p * 4:p * 4 + 4], VSG[par][:, pp, h * 128:(h + 1) * 128],
                       p == 0, False, [("AA", h), ("VSG", par)], [pk(obk[h])], inc=(pp == NPB - 1 and h == 1))
        for h in range(2):
            mm(PS[obk[h]][0:4, 0:128], ANW[0:4, h * 4:h * 4 + 4], VNB[0:4, h * 128:(h + 1) * 128], False, True,
               ["ANW", "VNB"], [pk(obk[h])])
            cp("dve", OSM[0:4, h * 128:(h + 1) * 128], PS[obk[h]][0:4, 0:128], [pk(obk[h])], ["OSM"])
        for h in range(2):
            S.op("pe", lambda h_, h=h: h_.transpose(PS[5][:, 0:4], OSM[0:4, h * 128:(h + 1) * 128],
                                                    C["identf"][0:4, 0:4]), ["OSM", ("c", "identf")], [pk(5)])
            headnorm_store(5, h, 4, dst_sample)
    S.coll(lambda g: g.collective_compute("AllGather", ALU.bypass, replica_groups=[[0, 1, 2, 3], [4, 5, 6, 7]],
                                          ins=[SRCS.ap().opt()], outs=[GATS.ap().opt()]),
           [("SRCS",)], [("GATS",)], ("ccs",))

    barrier()
    sstack.close()
    MTO = sb("MTO", [128, 16, 128], BF16)
    YY = sb("YY", [128, D], F32)
    FNW = sb("FNW", [128, D], F32)
    for ch in range(16):
        xt = XT[ch % 2]
        k = ("XT", ch % 2)
        dma("sp", xt[:], I["wout"][ch * 128:(ch + 1) * 128, :], [], [k], k)
        cp("dve" if ch % 2 == 0 else "pool", WC[:, ch, :], xt[:], [k], [("WC", ch)])
    dma("sp", FNW[:], I["fnw"], [], ["FNW"], "fnw")

    OIDX = sb("OIDX", [128, 16], I32)
    dma("sp", OIDX[:], I["oidx"], [], ["OIDX"], "oidx")

    def outproj(nt, gat_ap, ncol, xres_ap, y_ap, key):
        for ch in range(16):
            S.dma("pool", lambda h_, ch=ch: h_.indirect_dma_start(
                out=MTO[:, ch, 0:ncol], out_offset=None, in_=gat_ap,
                in_offset=bass.IndirectOffsetOnAxis(ap=OIDX[:, ch:ch + 1], axis=0)),
                ["OIDX", key], ["MTO"], "mto")
        dma("sp", XT[0][0:nt, :], xres_ap, [], [("XT", 0)], ("XT", 0))
        for g in range(4):
            b = g % 2
            for ch in range(16):
                mm(PS[b][0:nt, :], MTO[:, ch, 0:nt], WC[:, ch, g * 512:(g + 1) * 512], ch == 0, ch == 15,
                   ["MTO"] + WCK, [pk(b)])
            tt("dve", YY[0:nt, g * 512:(g + 1) * 512], PS[b][0:nt, :], XT[0][0:nt, g * 512:(g + 1) * 512], ALU.add,
               [pk(b), ("XT", 0)], [("YY", g)])
        YK = [("YY", g) for g in range(4)]
        S.op("act", lambda h: h.activation(out=XT[1][0:nt, :], in_=YY[0:nt, :], func=AF.Square,
                                           accum_out=st1[0:nt, 0:1]), YK, [("XT", 1), "st1"])
        act(st1[0:nt, 1:2], st1[0:nt, 0:1], AF.Sqrt, ["st1"], ["st1"], scale=1.0 / D, bias=EPS)
        rec(st1[0:nt, 2:3], st1[0:nt, 1:2], ["st1"], ["st1"])
        stt(XT[1][0:nt, :], YY[0:nt, :], st1[0:nt, 2:3], FNW[0:nt, :], ALU.mult, ALU.mult,
            YK + ["st1", "FNW"], [("XT", 1)])
        dma("sp", y_ap, XT[1][0:nt, :], [("XT", 1)], [], "yout")

    for it in range(ntiles_run if do_outproj else 0):
        outproj(128, GAT[it].ap(), 128, I["xres"][it * 128:(it + 1) * 128, :], O["yp"][it * 128:(it + 1) * 128, :],
                ("GAT", it))
    if do_outproj and do_sample:
        outproj(4, GATS.ap(), 16, I["xsres"], O["ys"], ("GATS",))

    S.wait_all("sp")

    with nc.Block() as block:
        @block.tensor
        def _(e):
            for f in S.eng["pe"]["ops"]:
                f(e)

        @block.scalar
        def _(e):
            for f in S.eng["act"]["ops"]:
                f(e)

        @block.vector
        def _(e):
            for f in S.eng["dve"]["ops"]:
                f(e)

        @block.gpsimd
        def _(e):
            for f in S.eng["pool"]["ops"]:
                f(e)

        @block.sync
        def _(e):
            for f in S.eng["sp"]["ops"]:
                f(e)
    return nc


_CACHE = {}


def kernel(x_prompt, x_sample, cache_k, cache_v, state_s, page_table, norm_w, w_in, gain_a, gain_b, sb_bias,
           lb_logits, w_out, final_norm_w):
    f32 = np.float32
    x_prompt = np.asarray(x_prompt, f32)
    x_sample = np.asarray(x_sample, f32)
    cache_k = np.asarray(cache_k, f32)
    cache_v = np.asarray(cache_v, f32)
    state_s = np.asarray(state_s, f32)
    page_table = np.asarray(page_table, np.int32)
    norm_w = np.asarray(norm_w, f32)
    w_in = np.asarray(w_in, f32)[0]
    w_out = np.asarray(w_out, f32)[0]
    gain_a = np.asarray(gain_a, f32)[0]
    gain_b = np.asarray(gain_b, f32)[0]
    sb_bias = np.asarray(sb_bias, f32)[0]
    lb_logits = np.asarray(lb_logits, f32)
    final_norm_w = np.asarray(final_norm_w, f32)

    if "nc" not in _CACHE:
        _CACHE["nc"] = build_program()
        _CACHE["consts"] = make_consts()
    nc = _CACHE["nc"]
    consts = _CACHE["consts"]

    WA = 1024
    in_maps = []
    ck_hg = [np.ascontiguousarray(cache_k[0][:, :, 2 * hg:2 * hg + 2, :]).reshape(NPH * 128, 256) for hg in range(4)]
    cv_hg = [np.ascontiguousarray(cache_v[0][:, :, 2 * hg:2 * hg + 2, :]).reshape(NPH * 128, 256) for hg in range(4)]
    rows = []
    for r in range(4):
        for hh in (2 * r, 2 * r + 1):
            rows.append(np.arange(hh * 128, hh * 128 + 128))
        for hh in (2 * r, 2 * r + 1):
            rows.append(WA + np.arange(hh * 128, hh * 128 + 128))
    wout_perm = np.ascontiguousarray(w_out[np.concatenate(rows)])
    fnw_rep = np.ascontiguousarray(np.broadcast_to(final_norm_w[None, :], (128, D)))
    normw_l = np.ascontiguousarray(norm_w[0].reshape(16, 128).T)
    for c in range(8):
        b, hg = c // 4, c % 4
        h0, h1 = 2 * hg, 2 * hg + 1

        def colsA(base, h):
            return np.arange(base * WA + h * 128, base * WA + h * 128 + 128)

        def colsB(base, h):
            return np.arange(4 * WA + base * WA + h * 128, 4 * WA + base * WA + h * 128 + 128)
        cols = np.concatenate([colsA(0, h0), colsA(0, h1), colsA(3, h0), colsA(3, h1),
                               colsB(0, h0), colsB(0, h1), colsB(3, h0), colsB(3, h1),
                               colsA(1, h0), colsA(1, h1), colsA(2, h0), colsA(2, h1),
                               colsB(1, h0), colsB(1, h1), colsB(2, h0), colsB(2, h1)])
        m = {}
        m["xfull"] = np.ascontiguousarray(x_prompt[b])
        m["xres"] = np.ascontiguousarray(
            x_prompt[b].reshape(NTILE, 4, 128, D)[:, hg].reshape(1024, D))
        m["xs4"] = np.ascontiguousarray(x_sample[4 * b:4 * b + 4].reshape(16, D))
        m["xsres"] = np.ascontiguousarray(x_sample[4 * b + hg])
        m["win"] = np.ascontiguousarray(w_in[:, cols])
        m["wout"] = wout_perm
        m["normw"] = normw_l
        m["gains"] = np.ascontiguousarray(np.stack([gain_a[h0 * 128:h0 * 128 + 128], gain_a[h1 * 128:h1 * 128 + 128],
                                                    gain_b[h0 * 128:h0 * 128 + 128], gain_b[h1 * 128:h1 * 128 + 128]],
                                                   axis=1))
        m["sbb"] = np.ascontiguousarray(np.broadcast_to(sb_bias[[h0, h1]][None, :], (128, 2)))
        m["lbl"] = np.ascontiguousarray(np.broadcast_to(lb_logits[None, :, h0 * 128:h0 * 128 + 256], (128, 2, 256)))
        m["fnw"] = fnw_rep
        m["ck"] = ck_hg[hg]
        m["cv"] = cv_hg[hg]
        m["pt"] = np.ascontiguousarray(np.broadcast_to(page_table[None, 4 * b:4 * b + 4, :], (128, 4, NPG)))
        m["st"] = np.ascontiguousarray(state_s[0, 4 * b:4 * b + 4, h0:h0 + 2])
        m["rank"] = np.array([[hg]], np.int32)
        pp_, rc_ = np.meshgrid(np.arange(128), np.arange(16), indexing="ij")
        m["oidx"] = ((((rc_ // 4) * 4 + hg) * 4 + (rc_ % 4)) * 128 + pp_).astype(np.int32)
        m.update(consts)
        in_maps.append(m)

    res = run_bass_kernel_spmd(nc, in_maps, core_ids=list(range(8)))
    R = res.results
    y_prompt = np.zeros((2, SEQ, D), f32)
    y_sample = np.zeros((8, 4, D), f32)
    k_p = np.zeros((1, 2, SEQ, 8, 128), f32)
    v_p = np.zeros((1, 2, SEQ, 8, 128), f32)
    s_p = np.zeros((1, 2, 8, 128, 128), f32)
    k_s = np.zeros((1, 8, 4, 8, 128), f32)
    v_s = np.zeros((1, 8, 4, 8, 128), f32)
    s_s = np.zeros((1, 8, 8, 128, 128), f32)
    for c in range(8):
        b, hg = c // 4, c % 4
        r = R[c]
        y_prompt[b].reshape(NTILE, 4, 128, D)[:, hg] = np.asarray(r["yp"]).reshape(NTILE, 128, D)
        y_sample[4 * b + hg] = np.asarray(r["ys"])
        k_p[0, b, :, 2 * hg:2 * hg + 2, :] = np.asarray(r["kn"]).reshape(SEQ, 2, 128)
        v_p[0, b, :, 2 * hg:2 * hg + 2, :] = np.asarray(r["vn"]).reshape(SEQ, 2, 128)
        s_p[0, b, 2 * hg:2 * hg + 2] = np.asarray(r["spn"])
        k_s[0, 4 * b:4 * b + 4, :, 2 * hg:2 * hg + 2, :] = np.asarray(r["ksn"]).reshape(4, 4, 2, 128)
        v_s[0, 4 * b:4 * b + 4, :, 2 * hg:2 * hg + 2, :] = np.asarray(r["vsn"]).reshape(4, 4, 2, 128)
        s_s[0, 4 * b:4 * b + 4, 2 * hg:2 * hg + 2] = np.asarray(r["ssn"])
    return (y_prompt, y_sample, k_p, v_p, s_p, k_s, v_s, s_s)
```

```python
import numpy as np
import ml_dtypes
import concourse.bass as bass
import concourse.mybir as mybir
from concourse.bass_utils import run_bass_kernel_spmd

F32 = mybir.dt.float32
BF16 = mybir.dt.bfloat16
I32 = mybir.dt.int32
AF = mybir.ActivationFunctionType
ALU = mybir.AluOpType

D = 2048
SEQ = 4096
NTILE = 8
TT = 512
NPH = 1280
NPG = 128
EPS = 1e-6
SCALE = 128 ** -0.5


class Sched:
    def __init__(self, nc):
        self.nc = nc
        self.eng = {}
        self.last_w = {}
        self.readers = {}
        self.dma_sems = {}
        self.n = 0
        self.limit = 10 ** 9
        self.marks = []
        self.log = []

    def mark(self, name):
        self.marks.append((name, self.n))

    def _skip(self):
        self.n += 1
        return self.n > self.limit

    def add_engine(self, name, same_engine_sync=True):
        self.eng[name] = dict(name=name, sem=self.nc.alloc_semaphore("sem_" + name), cnt=0,
                              waited={}, ops=[], same=same_engine_sync)

    def _deps(self, reads, writes):
        evs = []
        for k in reads:
            if k in self.last_w:
                evs.append(self.last_w[k])
            if isinstance(k, tuple) and k[0] == "PS":
                evs += self.readers.get(k, [])
        for k in writes:
            if k in self.last_w:
                evs.append(self.last_w[k])
            evs += self.readers.get(k, [])
        return evs

    def _wait(self, e, evs):
        need = {}
        for (sid, sem, val) in evs:
            if sid == e["name"] and not e["same"]:
                continue
            if e["waited"].get(sid, 0) < val:
                if sid not in need or need[sid][1] < val:
                    need[sid] = (sem, val)
        for sid, (sem, val) in need.items():
            e["waited"][sid] = val
            e["ops"].append(lambda h, sem=sem, val=val: h.wait_ge(sem, val))

    def _record(self, ev, reads, writes):
        for k in reads:
            self.readers.setdefault(k, []).append(ev)
        for k in writes:
            self.last_w[k] = ev
            self.readers[k] = []

    def op(self, engname, fn, reads=(), writes=(), inc=True):
        if self._skip():
            return
        self.log.append((self.n, engname, list(reads), list(writes)))
        e = self.eng[engname]
        self._wait(e, self._deps(reads, writes))
        if inc:
            e["cnt"] += 1
            sem = e["sem"]
            e["ops"].append(lambda h, fn=fn, sem=sem: fn(h).then_inc(sem, 1))
            ev = (e["name"], e["sem"], e["cnt"])
        else:
            e["ops"].append(lambda h, fn=fn: fn(h))
            ev = (e["name"], e["sem"], e["cnt"] + 1)
        self._record(ev, reads, writes)

    def dma(self, qname, fn, reads, writes, semkey):
        if self._skip():
            return
        self.log.append((self.n, "dma:" + qname, list(reads), list(writes)))
        e = self.eng[qname]
        self._wait(e, self._deps(reads, writes))
        if semkey not in self.dma_sems:
            self.dma_sems[semkey] = [self.nc.alloc_semaphore("dsem_%d" % len(self.dma_sems)), 0]
        s = self.dma_sems[semkey]
        s[1] += 16
        sem = s[0]
        e["ops"].append(lambda h, fn=fn, sem=sem: fn(h).then_inc(sem, 16))
        ev = ("dma_%s" % str(semkey), sem, s[1])
        self._record(ev, reads, writes)

    def coll(self, fn, reads, writes, semkey):
        if self._skip():
            return
        e = self.eng["pool"]
        self._wait(e, self._deps(reads, writes))
        if semkey not in self.dma_sems:
            self.dma_sems[semkey] = [self.nc.alloc_semaphore("csem_%d" % len(self.dma_sems)), 0]
        s = self.dma_sems[semkey]
        s[1] += 1
        sem = s[0]
        e["ops"].append(lambda h, fn=fn, sem=sem: fn(h).then_inc(sem))
        ev = ("coll_%s" % str(semkey), sem, s[1])
        self._record(ev, reads, writes)

    def wait_all(self, engname):
        e = self.eng[engname]
        evs = list(self.last_w.values())
        for r in self.readers.values():
            evs += r
        self._wait(e, evs)


def make_consts():
    c = {}
    c["identb"] = np.eye(128, dtype=np.float32).astype(ml_dtypes.bfloat16)
    j = np.arange(128)[:, None]
    s = np.arange(128)[None, :]
    c["negtri"] = (-(j >= s).astype(np.float32)).astype(ml_dtypes.bfloat16)
    c["negcomp"] = (-(j < s).astype(np.float32)).astype(ml_dtypes.bfloat16)
    c["negones"] = (-np.ones((128, 128), np.float32)).astype(ml_dtypes.bfloat16)
    c["onesm"] = (np.ones((128, 128), np.float32) / 128.0).astype(ml_dtypes.bfloat16)
    c["dmask"] = (j < s).astype(np.float32)
    c["imask"] = (j <= s).astype(np.float32)
    c["iota"] = np.arange(128, dtype=np.float32).reshape(128, 1)
    c["identf"] = np.eye(128, dtype=np.float32)
    c["onesf"] = np.ones((128, 128), np.float32)

    def hg_mats(T):
        mid = T // 2 - 1
        sp = np.arange(T)[:, None]
        ss = np.arange(T)[None, :]
        mtok = np.zeros((T, T), np.float32)
        mtok[(ss < sp) & (sp <= mid)] = 1.0
        mtok[(sp >= mid + 1) & (sp <= ss)] = -1.0
        mfeat = np.zeros((T, T + 2), np.float32)
        mfeat[:, :T] = -mtok
        mfeat[: mid + 1, T] = 1.0
        mfeat[mid + 1:, T + 1] = 1.0
        return mtok, mfeat
    c["mtok"], c["mfeat"] = hg_mats(128)
    c["mtok4"], c["mfeat4"] = hg_mats(4)
    return c


CONST_SPECS = [("identb", [128, 128], BF16), ("negtri", [128, 128], BF16), ("negcomp", [128, 128], BF16),
               ("negones", [128, 128], BF16), ("onesm", [128, 128], BF16), ("dmask", [128, 128], F32),
               ("imask", [128, 128], F32), ("iota", [128, 1], F32), ("mtok", [128, 128], F32),
               ("mfeat", [128, 130], F32), ("mtok4", [4, 4], F32), ("mfeat4", [4, 6], F32),
               ("identf", [128, 128], F32), ("onesf", [128, 128], F32)]

IN_SPECS = [("xfull", [-1, D], F32), ("xres", [1024, D], F32), ("xs4", [16, D], F32), ("xsres", [4, D], F32),
            ("win", [D, D], F32), ("wout", [D, D], F32), ("normw", [128, 16], F32), ("gains", [128, 4], F32),
            ("sbb", [128, 2], F32), ("lbl", [128, 2, 256], F32), ("fnw", [128, D], F32),
            ("ck", [None, 256], F32), ("cv", [None, 256], F32), ("pt", [128, 4, NPG], I32),
            ("st", [4, 2, 128, 128], F32), ("rank", [1, 1], I32), ("oidx", [128, 16], I32)]

OUT_SPECS = [("yp", [1024, D], F32), ("ys", [4, D], F32), ("kn", [SEQ, 256], F32), ("vn", [SEQ, 256], F32),
             ("spn", [2, 128, 128], F32), ("ksn", [16, 256], F32), ("vsn", [16, 256], F32),
             ("ssn", [4, 2, 128, 128], F32)]


def build_program(nph=NPH, ntiles_run=NTILE, do_sample=True, do_outproj=True, ndb=4, limit=None, xrows=SEQ):
    nc = bass.Bass("TRN2", target_bir_lowering=False)
    I = {}
    for name, shp, dt in IN_SPECS + CONST_SPECS:
        shp = [nph * 128 if d is None else (xrows if d == -1 else d) for d in shp]
        I[name] = nc.dram_tensor(name, shp, dt, kind="ExternalInput").ap()
    O = {}
    for name, shp, dt in OUT_SPECS:
        O[name] = nc.dram_tensor(name, shp, dt, kind="ExternalOutput").ap()
    SRC = [nc.dram_tensor("src%d" % i, [2048, 128], BF16) for i in range(NTILE)]
    GAT = [nc.dram_tensor("gat%d" % i, [8192, 128], BF16) for i in range(NTILE)]
    SRCS = nc.dram_tensor("srcs", [2048, 16], BF16)
    GATS = nc.dram_tensor("gats", [8192, 16], BF16)

    S = Sched(nc)
    if limit is not None:
        S.limit = limit
    nc._sched = S
    S.add_engine("pe", same_engine_sync=False)
    S.add_engine("act")
    S.add_engine("dve")
    S.add_engine("pool")
    S.add_engine("sp")

    def sb(name, shape, dt):
        return nc.alloc_sbuf_tensor("sb_" + name, shape, dt)

    C = {}
    for name, shp, dt in CONST_SPECS:
        C[name] = sb("c_" + name, shp, dt)
    WC = sb("WC", [128, 16, D], BF16)
    HT = sb("HT", [128, 16, TT], BF16)
    XT = [sb("XT%d" % i, [128, D], F32) for i in range(2)]
    XB = sb("XB", [128, D], BF16)
    normw = sb("normw", [128, 16], F32)
    gains = sb("gains", [128, 4], F32)
    sbb = sb("sbb", [128, 2], F32)
    lbl = sb("lbl", [128, 2, 256], F32)
    omlb = sb("omlb", [128, 256], F32)
    st1 = sb("st1", [128, 4], F32)
    QT = sb("QT", [128, 2, TT], BF16)
    ZS = sb("ZS", [128, 4, TT], F32)
    QB = sb("QB", [128, 2, TT], F32)
    KVS = [sb("KVS%d" % i, [128, 512], F32) for i in range(2)]
    KB = sb("KB", [128, 256], BF16)
    SN = sb("SN", [128, 256], F32)
    KK = sb("KK", [128, 256], F32)
    LG = sb("LG", [128, 256], F32)
    VB = sb("VB", [128, 256], BF16)
    EK = [sb("EK%d" % i, [128, 128], F32) for i in range(2)]
    KH = [sb("KH%d" % i, [128, 128], BF16) for i in range(2)]
    KHT = [sb("KHT%d" % i, [128, 128], BF16) for i in range(2)]
    EQ = [sb("EQ%d" % i, [128, 128], F32) for i in range(2)]
    QH = [sb("QH%d" % i, [128, 128], BF16) for i in range(2)]
    ECL = [sb("ECL%d" % i, [128, 4], F32) for i in range(2)]
    ATT = [sb("ATT%d" % i, [128, 128], BF16) for i in range(2)]
    SST = [sb("SST%d" % i, [128, 128], F32) for i in range(2)]
    SCB = [sb("SCB%d" % i, [128, 128], BF16) for i in range(2)]
    SL = [sb("SL%d" % i, [128, 128], F32) for i in range(2)]
    EE = [sb("EE%d" % i, [128, TT], F32) for i in range(2)]
    SPB = [sb("SPB%d" % i, [128, TT], BF16) for i in range(2)]
    XX = [sb("XX%d" % i, [128, TT], F32) for i in range(2)]
    AA = [sb("AA%d" % i, [128, TT], BF16) for i in range(2)]
    EE2 = [sb("EE2%d" % i, [128, TT], F32) for i in range(2)]
    SPB2 = [sb("SPB2%d" % i, [128, TT], BF16) for i in range(2)]
    XX2 = [sb("XX2%d" % i, [128, TT], F32) for i in range(2)]
    AA2 = [sb("AA2%d" % i, [128, TT], BF16) for i in range(2)]
    OSQ = sb("OSQ", [128, TT], BF16)
    OSB = sb("OSB", [128, TT], F32)
    RIN = sb("RIN", [128, TT], F32)
    MT = sb("MT", [128, TT], BF16)
    SC = sb("SC", [128, 4], F32)
    ENW = sb("ENW", [4, 8], F32)
    SPN = sb("SPN", [4, 4], BF16)
    ANW = sb("ANW", [4, 8], BF16)
    OSM = sb("OSM", [4, 256], F32)
    PJS = sb("PJS", [4, 512], F32)
    KNT = sb("KNT", [128, 2, 4], BF16)
    VNB = sb("VNB", [4, 256], BF16)
    RK = sb("RK", [1, 1], I32)
    NPB = 8

    from contextlib import ExitStack
    pstack = ExitStack()
    KT = pstack.enter_context(nc.sbuf_tensor("sb_KT", [128, 2, SEQ], BF16))
    VV = pstack.enter_context(nc.sbuf_tensor("sb_VV", [128, 32, 256], BF16))
    PS = [nc.alloc_psum_tensor("ps%d" % i, [128, 512], F32) for i in range(8)]
    PSB = [p[:].bitcast(BF16) for p in PS]

    def pk(b):
        return ("PS", b)

    def dma(q, out, in_, reads, writes, key, **kw):
        S.dma(q, lambda h: h.dma_start(out=out, in_=in_, **kw), reads, writes, key)

    def mm(out, lhsT, rhs, start, stop, reads, writes, inc=None):
        if inc is None:
            inc = stop
        S.op("pe", lambda h: h.matmul(out, lhsT, rhs, start=start, stop=stop), reads, writes, inc=inc)

    def tr(out, in_, ident, reads, writes, inc=True):
        S.op("pe", lambda h: h.transpose(out, in_, ident), reads, writes, inc=inc)

    def act(out, in_, func, reads, writes, **kw):
        S.op("act", lambda h: h.activation(out=out, in_=in_, func=func, **kw), reads, writes)

    def tt(eng, out, in0, in1, op, reads, writes):
        S.op(eng, lambda h: h.tensor_tensor(out=out, in0=in0, in1=in1, op=op), reads, writes)

    def ts(eng, out, in0, s1, s2, op0, op1, reads, writes):
        if op1 is None:
            S.op(eng, lambda h: h.tensor_scalar(out=out, in0=in0, scalar1=s1, scalar2=None, op0=op0), reads, writes)
        else:
            S.op(eng, lambda h: h.tensor_scalar(out=out, in0=in0, scalar1=s1, scalar2=s2, op0=op0, op1=op1),
                 reads, writes)

    def stt(out, in0, sc, in1, op0, op1, reads, writes):
        S.op("dve", lambda h: h.scalar_tensor_tensor(out=out, in0=in0, scalar=sc, in1=in1, op0=op0, op1=op1),
             reads, writes)

    def cp(eng, out, in_, reads, writes):
        if eng == "act_copy":
            act(out, in_, AF.Copy, reads, writes)
        else:
            S.op(eng, lambda h: h.tensor_copy(out=out, in_=in_), reads, writes)

    def rec(out, in_, reads, writes):
        S.op("dve", lambda h: h.reciprocal(out=out, in_=in_), reads, writes)

    for name, shp, dt in CONST_SPECS:
        dma("sp", C[name][:], I[name], [], [("c", name)], ("c", name))
    ck_all = [("c", n) for n, _, _ in CONST_SPECS]
    dma("sp", normw[:], I["normw"], [], ["normw"], "normw")
    dma("sp", gains[:], I["gains"], [], ["gains"], "gains")
    dma("sp", sbb[:], I["sbb"], [], ["sbb"], "sbb")
    dma("sp", lbl[:], I["lbl"], [], ["lbl"], "lbl")
    dma("sp", RK[:], I["rank"], [], ["RK"], "RK")
    tt("dve", omlb[:], lbl[:, 1, :], lbl[:, 0, :], ALU.subtract, ["lbl"], ["omlb"])
    act(omlb[:], omlb[:], AF.Sigmoid, ["omlb"], ["omlb"])

    for ch in range(16):
        xt = XT[ch % 2]
        k = ("XT", ch % 2)
        dma("sp", xt[:], I["win"][ch * 128:(ch + 1) * 128, :], [], [k], k)
        eng = "dve" if ch % 2 == 0 else "pool"
        ts(eng, WC[:, ch, :], xt[:], normw[:, ch:ch + 1], None, ALU.mult, None, [k, "normw"], [("WC", ch)])
    WCK = [("WC", ch) for ch in range(16)]

    def load_norm_block(src_ap, nt, col0, par):
        xt = XT[par]
        k = ("XT", par)
        dma("sp", xt[0:nt, :], src_ap, [], [k], k)
        S.op("act", lambda h: h.activation(out=XB[0:nt, :], in_=xt[0:nt, :], func=AF.Square,
                                           accum_out=st1[0:nt, 0:1]), [k], ["XB", "st1"])
        act(st1[0:nt, 1:2], st1[0:nt, 0:1], AF.Sqrt, ["st1"], ["st1"], scale=1.0 / D, bias=EPS)
        rec(st1[0:nt, 2:3], st1[0:nt, 1:2], ["st1"], ["st1"])
        ts("dve", XB[0:nt, 0:1024], xt[0:nt, 0:1024], st1[0:nt, 2:3], None, ALU.mult, None, [k, "st1"], ["XB"])
        ts("pool", XB[0:nt, 1024:2048], xt[0:nt, 1024:2048], st1[0:nt, 2:3], None, ALU.mult, None,
           [k, "st1"], ["XBb"])
        for g in range(4):
            b = 6 + (g % 2)
            for c4 in range(4):
                ch = g * 4 + c4
                tr(PSB[b][:, c4 * 128:c4 * 128 + nt], XB[0:nt, ch * 128:(ch + 1) * 128], C["identb"][0:nt, 0:nt],
                   ["XB", "XBb", ("c", "identb")], [pk(b)], inc=(c4 == 3))
            src = PSB[b][:, 0:512].rearrange("p (c t) -> p c t", c=4)[:, :, 0:nt]
            cp("dve" if g % 2 == 0 else "act_copy", HT[:, g * 4:(g + 1) * 4, col0:col0 + nt], src, [pk(b)],
               [("HT", g)])

    HTK = [("HT", g) for g in range(4)]

    def inproj_feature(nt):
        for fb in range(8):
            b = fb % 2
            for ch in range(16):
                mm(PS[b][:, 0:nt], WC[:, ch, fb * 128:(fb + 1) * 128], HT[:, ch, 0:nt], ch == 0, ch == 15,
                   WCK + HTK, [pk(b)])
            if fb < 2:
                cp("dve", QT[:, fb, 0:nt], PS[b][:, 0:nt], [pk(b)], [("QT", fb)])
            elif fb < 4:
                act(ZS[:, fb - 2, 0:nt], PS[b][:, 0:nt], AF.Silu, [pk(b)], [("ZS", fb - 2)])
            elif fb < 6:
                act(QB[:, fb - 4, 0:nt], PS[b][:, 0:nt], AF.Silu, [pk(b)], [("QB", fb - 4)])
            else:
                act(ZS[:, fb - 4, 0:nt], PS[b][:, 0:nt], AF.Silu, [pk(b)], [("ZS", fb - 4)])

    def inproj_token(nt, c0, grp, b):
        for ch in range(16):
            mm(PS[b][0:nt, :], HT[:, ch, c0:c0 + nt], WC[:, ch, 1024 + grp * 512:1024 + (grp + 1) * 512],
               ch == 0, ch == 15, WCK + HTK, [pk(b)])

    def headnorm_store(ob, slot, nt, dst_fn):
        act(OSQ[:, 0:nt], PS[ob][:, 0:nt], AF.Square, [pk(ob)], ["OSQ"])
        cp("dve", OSB[:, 0:nt], PS[ob][:, 0:nt], [pk(ob)], ["OSB"])
        mm(PS[0][:, 0:nt], C["onesm"][:], OSQ[:, 0:nt], True, True, ["OSQ", ("c", "onesm")], [pk(0)])
        act(RIN[:, 0:nt], PS[0][:, 0:nt], AF.Sqrt, [pk(0)], ["RIN"], bias=EPS)
        rec(RIN[:, 0:nt], RIN[:, 0:nt], ["RIN"], ["RIN"])
        tt("dve", OSB[:, 0:nt], OSB[:, 0:nt], RIN[:, 0:nt], ALU.mult, ["OSB", "RIN"], ["OSB"])
        stt(MT[:, 0:nt], ZS[:, slot, 0:nt], gains[:, slot:slot + 1], OSB[:, 0:nt], ALU.mult, ALU.mult,
            [("ZS", slot), "gains", "OSB"], ["MT"])
        dst_fn(slot)

    def emit_interleaved(lists):
        n = max(len(l) for l in lists)
        for i in range(n):
            for l in lists:
                if i < len(l):
                    l[i]()

    def hgrn_stages(T, hb, qcol, ob, oc, mtok, mfeat, mk):
        hs = slice(hb * 128, (hb + 1) * 128)
        X, Y = 2 + 2 * hb, 3 + 2 * hb
        ek, kh, kht, eq, qh, ecl, att, scb, sl = (EK[hb], KH[hb], KHT[hb], EQ[hb], QH[hb], ECL[hb], ATT[hb],
                                                  SCB[hb], SL[hb])
        K_ = lambda n: (n, hb)
        sk = ("SST", hb)
        cmb = PS[X][0:T, 0:128]
        bmc = PS[X][:, 128:128 + T + 2]
        dsp = PS[X][:, 384:512]
        khp = PSB[Y][:, 0:T]
        atp = PS[Y][0:T, 256:256 + T]
        st = []
        st.append(lambda: mm(cmb, mtok, LG[0:T, hs], True, True, ["LG"] + mk, [pk(X)]))
        st.append(lambda: mm(bmc, LG[0:T, hs], mfeat, True, True, ["LG"] + mk, [pk(X)]))
        st.append(lambda: act(ek[0:T, :], cmb, AF.Exp, [pk(X)], [K_("EK")]))
        st.append(lambda: tt("dve", kh[0:T, :], KK[0:T, hs], ek[0:T, :], ALU.mult, ["KK", K_("EK")], [K_("KH")]))
        st.append(lambda: act(eq[:, 0:T], PS[X][:, 128:128 + T], AF.Exp, [pk(X)], [K_("EQ")]))
        st.append(lambda: tt("dve", qh[:, 0:T], QB[:, hb, qcol:qcol + T], eq[:, 0:T], ALU.mult,
                             [("QB", hb), K_("EQ")], [K_("QH")]))
        st.append(lambda: act(ecl[:, 0:2], PS[X][:, 128 + T:128 + T + 2], AF.Exp, [pk(X)], [K_("ECL")]))
        st.append(lambda: tt("dve", ecl[:, 2:3], ecl[:, 0:1], ecl[:, 1:2], ALU.mult, [K_("ECL")], [K_("ECL")]))
        st.append(lambda: tr(khp, kh[0:T, :], C["identb"][0:T, 0:T], [K_("KH"), ("c", "identb")], [pk(Y)]))
        st.append(lambda: cp("dve", kht[:, 0:T], khp, [pk(Y)], [K_("KHT")]))
        st.append(lambda: mm(atp, kht[:, 0:T], qh[:, 0:T], True, True, [K_("KHT"), K_("QH")], [pk(Y)]))
        st.append(lambda: ts("dve", ek[0:T, 0:T], atp, 1e30, -1e30, ALU.min, ALU.max, [pk(Y), K_("EK")],
                             [K_("EK")]))
        st.append(lambda: tt("dve", att[0:T, 0:T], ek[0:T, 0:T], C["imask"][0:T, 0:T], ALU.mult,
                             [K_("EK"), ("c", "imask")], [K_("ATT")]))
        st.append(lambda: ts("dve", scb[:], SST[hb][:], ecl[:, 0:1], None, ALU.mult, None, [sk, K_("ECL")],
                             [K_("SCB")]))
        st.append(lambda: ts("pool", sl[:], SST[hb][:], ecl[:, 2:3], None, ALU.mult, None, [sk, K_("ECL")],
                             [K_("SL")]))

        def omm():
            mm(PS[ob][:, oc:oc + T], scb[:], qh[:, 0:T], True, False, [K_("SCB"), K_("QH")], [pk(ob)], inc=False)
            mm(PS[ob][:, oc:oc + T], VB[0:T, hs], att[0:T, 0:T], False, True, ["VB", K_("ATT")], [pk(ob)])
        st.append(omm)
        st.append(lambda: mm(dsp, kh[0:T, :], VB[0:T, hs], True, True, [K_("KH"), "VB"], [pk(X)]))
        st.append(lambda: stt(SST[hb][:], dsp, ecl[:, 1:2], sl[:], ALU.mult, ALU.add, [pk(X), K_("ECL"), K_("SL")],
                              [sk]))
        return st

    def hgrn_gates(T, b):
        act(SN[0:T, :], PS[b][0:T, 0:256], AF.Sigmoid, [pk(b)], ["SN"], scale=-1.0)
        cp("dve", VB[0:T, :], PS[b][0:T, 256:512], [pk(b)], ["VB"])
        tt("dve", KK[0:T, :], SN[0:T, :], omlb[0:T, :], ALU.mult, ["SN", "omlb"], ["KK"])
        act(LG[0:T, :], KK[0:T, :], AF.Ln, ["KK"], ["LG"], scale=-1.0, bias=1.0)

    mk128 = [("c", "mtok"), ("c", "mfeat")]
    mk4 = [("c", "mtok4"), ("c", "mfeat4")]

    for hb in range(2):
        S.op("pool", lambda h, hb=hb: h.memset(SST[hb][:], 0.0), [], [("SST", hb)])

    for it in range(ntiles_run):
        t0 = it * TT
        S.mark('tile%d_x' % it)
        for blk in range(4):
            load_norm_block(I["xfull"][t0 + blk * 128:t0 + (blk + 1) * 128, :], 128, blk * 128, blk % 2)
        S.mark('inproj_feat')
        inproj_feature(TT)
        S.mark('groupA')
        for blk in range(4):
            gb = it * 4 + blk
            b = blk % 2
            inproj_token(128, blk * 128, 0, b)
            kv = KVS[blk % 2]
            kk_ = ("KVS", blk % 2)
            cp("act_copy", kv[:], PS[b][:, :], [pk(b)], [kk_])
            dma("sp", O["kn"][t0 + blk * 128:t0 + (blk + 1) * 128, :], kv[:, 0:256], [kk_], [], ("okn", blk % 2))
            dma("sp", O["vn"][t0 + blk * 128:t0 + (blk + 1) * 128, :], kv[:, 256:512], [kk_], [], ("ovn", blk % 2))
            cp("pool", VV[:, gb, :], kv[:, 256:512], [kk_], [("VV", gb)])
            cp("pool", KB[:], kv[:, 0:256], [kk_], ["KB"])
            for h in range(2):
                tr(PSB[7][:, h * 128:(h + 1) * 128], KB[:, h * 128:(h + 1) * 128], C["identb"][:],
                   ["KB", ("c", "identb")], [pk(7)], inc=(h == 1))
            cp("dve", KT[:, :, gb * 128:(gb + 1) * 128], PSB[7][:, 0:256].rearrange("p (h t) -> p h t", h=2),
               [pk(7)], [("KT", gb)])
        S.mark('hgrn')
        for blk in range(4):
            b = 0
            inproj_token(128, blk * 128, 1, b)
            hgrn_gates(128, b)
            emit_interleaved([hgrn_stages(128, hb, blk * 128, 7 if hb == 0 else 1, blk * 128, C["mtok"][:],
                                          C["mfeat"][:], mk128) for hb in range(2)])
        S.mark('headnormB')
        def dst_prompt(slot, it=it):
            dma("sp", SRC[it].ap().rearrange("(j f) t -> f j t", j=4)[slot * 128:(slot + 1) * 128, :, :],
                MT[:].rearrange("p (j t) -> p j t", j=4), ["MT"], [("SRC", it)], ("src", it))
        headnorm_store(7, 2, TT, dst_prompt)
        headnorm_store(1, 3, TT, dst_prompt)
        S.mark('attn')
        nkb = 4 * it + 4
        bufs = {"EE": (EE, EE2), "SPB": (SPB, SPB2), "XX": (XX, XX2), "AA": (AA, AA2)}

        def stages(kb, h):
            j = kb - 4 * it
            qs = 128 * j if j > 0 else 0
            qsl = slice(qs, TT)
            p = kb % 2
            zb, tb, ob = 2 + h, 4 + h, 6 + h
            first = (kb == nkb - 1)

            def B(n):
                return bufs[n][p][h], (n if p == 0 else n + "2", h)
            ee, eek = B("EE")
            spb, spk = B("SPB")
            xx, xxk = B("XX")
            aa, aak = B("AA")
            front = []
            front.append(lambda: mm(PS[zb][:, qsl], KT[:, h, kb * 128:(kb + 1) * 128], QT[:, h, qsl], True, True,
                                    [("KT", kb), ("QT", h)], [pk(zb)]))
            front.append(lambda: act(ee[:, qsl], PS[zb][:, qsl], AF.Exp, [pk(zb), "sbb"], [eek], scale=SCALE,
                                     bias=sbb[:, h:h + 1]))
            if j >= 0:
                front.append(lambda: tt("dve", ee[:, qs:qs + 128], ee[:, qs:qs + 128], C["dmask"][:], ALU.mult,
                                        [eek, ("c", "dmask")], [eek]))
            else:
                front.append(lambda: None)
            front.append(lambda: act(spb[:, qsl], ee[:, qsl], AF.Ln, [eek], [spk], bias=1.0))
            back = []
            back.append(lambda: mm(PS[tb][:, qsl], C["negtri"][:], spb[:, qsl], first, True,
                                   [spk, ("c", "negtri")], [pk(tb)]))
            back.append(lambda: act(xx[:, qsl], PS[tb][:, qsl], AF.Exp, [pk(tb)], [xxk]))
            if kb > 0:
                back.append(lambda: mm(PS[tb][:, qsl], C["negcomp"][:], spb[:, qsl], False, True,
                                       [spk, ("c", "negcomp"), xxk], [pk(tb)]))
            else:
                back.append(lambda: None)
            back.append(lambda: tt("dve", aa[:, qsl], ee[:, qsl], xx[:, qsl], ALU.mult, [eek, xxk], [aak]))
            back.append(lambda: mm(PS[ob][:, qsl], VV[:, kb, h * 128:(h + 1) * 128], aa[:, qsl], first, True,
                                   [("VV", kb), aak], [pk(ob)]))
            return front, back

        prev_back = None
        for kb in range(nkb - 1, -1, -1):
            st = [stages(kb, h) for h in range(2)]
            emit_interleaved([st[0][0], st[1][0]])
            if prev_back is not None:
                emit_interleaved(prev_back)
            prev_back = [st[0][1], st[1][1]]
        emit_interleaved(prev_back)
        for h in range(2):
            headnorm_store(6 + h, h, TT, dst_prompt)
        S.mark('allgather')
        S.coll(lambda g, it=it: g.collective_compute("AllGather", ALU.bypass,
                                                     replica_groups=[[0, 1, 2, 3], [4, 5, 6, 7]],
                                                     ins=[SRC[it].ap().opt()], outs=[GAT[it].ap().opt()]),
               [("SRC", it)], [("GAT", it)], ("cc", it))

    for hb in range(2):
        dma("sp", O["spn"][hb], SST[hb][:], [("SST", hb)], [], ("ospn", hb))

    def barrier():
        for en in ("pe", "act", "dve", "pool", "sp"):
            S.wait_all(en)
    barrier()
    pstack.close()
    sstack = ExitStack()
    PTS = sstack.enter_context(nc.sbuf_tensor("sb_PTS", [128, 4, NPG], I32))
    IDX = sstack.enter_context(nc.sbuf_tensor("sb_IDX", [128, 4, NPG], I32))
    PST = [sstack.enter_context(nc.sbuf_tensor("sb_PST%d" % i, [128, NPB, 256], F32)) for i in range(2)]
    VSG = [sstack.enter_context(nc.sbuf_tensor("sb_VSG%d" % i, [128, NPB, 256], BF16)) for i in range(2)]
    KTP = [sstack.enter_context(nc.sbuf_tensor("sb_KTP%d" % i, [128, 2, NPB * 128], BF16)) for i in range(2)]

    dma("sp", PTS[:], I["pt"], [], ["PTS"], "PTS")
    S.op("dve", lambda h: h.tensor_scalar(out=IDX[:], in0=PTS[:], scalar1=128.0, scalar2=C["iota"][:, 0:1],
                                          op0=ALU.mult, op1=ALU.add), ["PTS", ("c", "iota")], ["IDX"])
    ngrp = NPG // NPB
    for db in range(ndb if do_sample else 0):
        load_norm_block(I["xs4"][db * 4:(db + 1) * 4, :], 4, 0, db % 2)
        inproj_feature(4)
        inproj_token(4, 0, 0, 0)
        cp("act_copy", PJS[0:4, 0:512], PS[0][0:4, :], [pk(0)], ["PJS"])
        dma("sp", O["ksn"][db * 4:(db + 1) * 4, :], PJS[0:4, 0:256], ["PJS"], [], "oksn")
        dma("sp", O["vsn"][db * 4:(db + 1) * 4, :], PJS[0:4, 256:512], ["PJS"], [], "ovsn")
        cp("dve", VNB[0:4, :], PJS[0:4, 256:512], ["PJS"], ["VNB"])
        cp("dve", KB[0:4, :], PJS[0:4, 0:256], ["PJS"], ["KB"])
        for h in range(2):
            tr(PSB[7][:, h * 4:(h + 1) * 4], KB[0:4, h * 128:(h + 1) * 128], C["identb"][0:4, 0:4],
               ["KB", ("c", "identb")], [pk(7)], inc=(h == 1))
        cp("dve", KNT[:, :, :], PSB[7][:, 0:8].rearrange("p (h t) -> p h t", h=2), [pk(7)], ["KNT"])
        for hb in range(2):
            dma("sp", SST[hb][:], I["st"][db, hb], [], [("SST", hb)], ("sst", hb))
        inproj_token(4, 0, 1, 0)
        hgrn_gates(4, 0)
        emit_interleaved([hgrn_stages(4, hb, 0, 7 if hb == 0 else 1, 0, C["mtok4"][:], C["mfeat4"][:], mk4)
                          for hb in range(2)])
        for hb in range(2):
            dma("sp", O["ssn"][db, hb], SST[hb][:], [("SST", hb)], [], ("ossn", hb))

        def dst_sample(slot, db=db):
            dma("sp", SRCS.ap().rearrange("(j f) t -> f j t", j=4)[slot * 128:(slot + 1) * 128, db, 0:4],
                MT[:, 0:4], ["MT"], [("SRCS",)], "srcs")
        headnorm_store(7, 2, 4, dst_sample)
        headnorm_store(1, 3, 4, dst_sample)
        for g in range(ngrp):
            par = g % 2
            for pp in range(NPB):
                p = g * NPB + pp
                S.dma("pool", lambda h_, par=par, pp=pp, p=p, db=db: h_.indirect_dma_start(
                    out=PST[par][:, pp, :], out_offset=None, in_=I["ck"],
                    in_offset=bass.IndirectOffsetOnAxis(ap=IDX[:, db, p:p + 1], axis=0)),
                    ["IDX"], [("PST", par)], ("pst", par))
            for q4 in range(NPB // 4):
                for h in range(2):
                    for i4 in range(4):
                        pp = q4 * 4 + i4
                        S.op("pe", lambda h_, par=par, pp=pp, h=h, i4=i4: h_.transpose(
                            PS[6 + h][:, i4 * 128:(i4 + 1) * 128], PST[par][:, pp, h * 128:(h + 1) * 128],
                            C["identf"][:]), [("PST", par), ("c", "identf")], [pk(6 + h)], inc=(i4 == 3))
                    cp("dve" if h == 0 else "act_copy", KTP[par][:, h, q4 * 512:(q4 + 1) * 512], PS[6 + h][:, :],
                       [pk(6 + h)], [("KTP", par, h, q4)])
            for pp in range(NPB):
                p = g * NPB + pp
                for h in range(2):
                    mm(PS[2 + h][:, p * 4:p * 4 + 4], KTP[par][:, h, pp * 128:(pp + 1) * 128], QT[:, h, 0:4],
                       True, True, [("KTP", par, h, pp // 4), ("QT", h)], [pk(2 + h)],
                       inc=(pp == NPB - 1 and h == 1))
        for h in range(2):
            zb = 2 + h
            act(EE[h][:, :], PS[zb][:, :], AF.Exp, [pk(zb), "sbb"], [("EE", h)], scale=SCALE, bias=sbb[:, h:h + 1])
            act(SPB[h][:, :], EE[h][:, :], AF.Ln, [("EE", h)], [("SPB", h)], bias=1.0)
            mm(PS[0][:, :], C["negones"][:], SPB[h][:, :], True, True, [("SPB", h), ("c", "negones")], [pk(0)])
            cp("dve", XX[0][:, :], PS[0][:, :], [pk(0)], [("XX", 0)])
            x0v = XX[0][:].rearrange("p (a q) -> p q a", q=4)
            x1v = XX[1][:].rearrange("p (a q) -> p q a", q=4)
            for q in range(4):
                S.op("dve", lambda h_, q=q, x0v=x0v, x1v=x1v: h_.tensor_tensor_scan(
                    out=x1v[:, q, :], data0=C["onesf"][:, 0:NPG], data1=x0v[:, q, :], initial=0.0,
                    op0=ALU.mult, op1=ALU.add), [("XX", 0), ("c", "onesf")], [("XX", 1)])
            mm(PS[1][0:4, 0:4], KNT[:, h, :], QT[:, h, 0:4], True, True, ["KNT", ("QT", h)], [pk(1)])
            act(ENW[0:4, 0:4], PS[1][0:4, 0:4], AF.Exp, [pk(1), "sbb"], ["ENW"], scale=SCALE,
                bias=sbb[0:4, h:h + 1])
            tt("dve", ENW[0:4, 0:4], ENW[0:4, 0:4], C["dmask"][0:4, 0:4], ALU.mult, ["ENW", ("c", "dmask")], ["ENW"])
            act(SPN[0:4, 0:4], ENW[0:4, 0:4], AF.Ln, ["ENW"], ["SPN"], bias=1.0)
            mm(PS[1][:, 4:8], C["negones"][0:4, :], SPN[0:4, 0:4], True, True, ["SPN", ("c", "negones")], [pk(1)])
            tt("dve", SC[:, 0:4], XX[1][:, 508:512], PS[1][:, 4:8], ALU.add, [("XX", 1), pk(1)], ["SC"])
            for q in range(4):
                ts("dve", x0v[:, q, :], x1v[:, q, :], -1.0, SC[:, q:q + 1], ALU.mult, ALU.add,
                   [("XX", 1), "SC"], [("XX", 0)])
            mm(PS[4][:, :], C["negtri"][:], SPB[h][:, :], True, True, [("SPB", h), ("c", "negtri")], [pk(4)])
            tt("dve", XX[1][:, :], PS[4][:, :], XX[0][:, :], ALU.add, [pk(4), ("XX", 0)], [("XX", 1)])
            act(XX[1][:, :], XX[1][:, :], AF.Exp, [("XX", 1)], [("XX", 1)])
            tt("dve", AA[h][:, :], EE[h][:, :], XX[1][:, :], ALU.mult, [("EE", h), ("XX", 1)], [("AA", h)])
            mm(PS[1][0:4, 8:12], C["negtri"][0:4, 0:4], SPN[0:4, 0:4], True, True, ["SPN", ("c", "negtri")], [pk(1)])
            act(ENW[0:4, 4:8], PS[1][0:4, 8:12], AF.Exp, [pk(1)], ["ENW"])
            tt("dve", ANW[0:4, h * 4:h * 4 + 4], ENW[0:4, 0:4], ENW[0:4, 4:8], ALU.mult, ["ENW"], ["ANW"])
        obk = [5, 1]
        for g in range(ngrp):
            par = g % 2
            for pp in range(NPB):
                p = g * NPB + pp
                S.dma("pool", lambda h_, par=par, pp=pp, p=p, db=db: h_.indirect_dma_start(
                    out=PST[par][:, pp, :], out_offset=None, in_=I["cv"],
                    in_offset=bass.IndirectOffsetOnAxis(ap=IDX[:, db, p:p + 1], axis=0)),
                    ["IDX"], [("PST", par)], ("pst", par))
            cp("dve" if g % 2 == 0 else "act_copy", VSG[par][:, :, :], PST[par][:, :, :], [("PST", par)],
               [("VSG", par)])
            for pp in range(NPB):
                p = g * NPB + pp
                for h in range(2):
                    mm(PS[obk[h]][0:4, 0:128], AA[h][:, p * 4:p * 4 + 4], VSG[par][:, pp, h * 128:(h + 1) * 128],
                       p == 0, False, [("AA", h), ("VSG", par)], [pk(obk[h])], inc=(pp == NPB - 1 and h == 1))
        for h in range(2):
            mm(PS[obk[h]][0:4, 0:128], ANW[0:4, h * 4:h * 4 + 4], VNB[0:4, h * 128:(h + 1) * 128], False, True,
               ["ANW", "VNB"], [pk(obk[h])])
            cp("dve", OSM[0:4, h * 128:(h + 1) * 128], PS[obk[h]][0:4, 0:128], [pk(obk[h])], ["OSM"])
        for h in range(2):
            S.op("pe", lambda h_, h=h: h_.transpose(PS[5][:, 0:4], OSM[0:4, h * 128:(h + 1) * 128],
                                                    C["identf"][0:4, 0:4]), ["OSM", ("c", "identf")], [pk(5)])
            headnorm_store(5, h, 4, dst_sample)
    S.coll(lambda g: g.collective_compute("AllGather", ALU.bypass, replica_groups=[[0, 1, 2, 3], [4, 5, 6, 7]],
                                          ins=[SRCS.ap().opt()], outs=[GATS.ap().opt()]),
           [("SRCS",)], [("GATS",)], ("ccs",))

    barrier()
    sstack.close()
    MTO = sb("MTO", [128, 16, 128], BF16)
    YY = sb("YY", [128, D], F32)
    FNW = sb("FNW", [128, D], F32)
    for ch in range(16):
        xt = XT[ch % 2]
        k = ("XT", ch % 2)
        dma("sp", xt[:], I["wout"][ch * 128:(ch + 1) * 128, :], [], [k], k)
        cp("dve" if ch % 2 == 0 else "pool", WC[:, ch, :], xt[:], [k], [("WC", ch)])
    dma("sp", FNW[:], I["fnw"], [], ["FNW"], "fnw")

    OIDX = sb("OIDX", [128, 16], I32)
    dma("sp", OIDX[:], I["oidx"], [], ["OIDX"], "oidx")

    def outproj(nt, gat_ap, ncol, xres_ap, y_ap, key):
        for ch in range(16):
            S.dma("pool", lambda h_, ch=ch: h_.indirect_dma_start(
                out=MTO[:, ch, 0:ncol], out_offset=None, in_=gat_ap,
                in_offset=bass.IndirectOffsetOnAxis(ap=OIDX[:, ch:ch + 1], axis=0)),
                ["OIDX", key], ["MTO"], "mto")
        dma("sp", XT[0][0:nt, :], xres_ap, [], [("XT", 0)], ("XT", 0))
        for g in range(4):
            b = g % 2
            for ch in range(16):
                mm(PS[b][0:nt, :], MTO[:, ch, 0:nt], WC[:, ch, g * 512:(g + 1) * 512], ch == 0, ch == 15,
                   ["MTO"] + WCK, [pk(b)])
            tt("dve", YY[0:nt, g * 512:(g + 1) * 512], PS[b][0:nt, :], XT[0][0:nt, g * 512:(g + 1) * 512], ALU.add,
               [pk(b), ("XT", 0)], [("YY", g)])
        YK = [("YY", g) for g in range(4)]
        S.op("act", lambda h: h.activation(out=XT[1][0:nt, :], in_=YY[0:nt, :], func=AF.Square,
                                           accum_out=st1[0:nt, 0:1]), YK, [("XT", 1), "st1"])
        act(st1[0:nt, 1:2], st1[0:nt, 0:1], AF.Sqrt, ["st1"], ["st1"], scale=1.0 / D, bias=EPS)
        rec(st1[0:nt, 2:3], st1[0:nt, 1:2], ["st1"], ["st1"])
        stt(XT[1][0:nt, :], YY[0:nt, :], st1[0:nt, 2:3], FNW[0:nt, :], ALU.mult, ALU.mult,
            YK + ["st1", "FNW"], [("XT", 1)])
        dma("sp", y_ap, XT[1][0:nt, :], [("XT", 1)], [], "yout")

    for it in range(ntiles_run if do_outproj else 0):
        outproj(128, GAT[it].ap(), 128, I["xres"][it * 128:(it + 1) * 128, :], O["yp"][it * 128:(it + 1) * 128, :],
                ("GAT", it))
    if do_outproj and do_sample:
        outproj(4, GATS.ap(), 16, I["xsres"], O["ys"], ("GATS",))

    S.wait_all("sp")

    with nc.Block() as block:
        @block.tensor
        def _(e):
            for f in S.eng["pe"]["ops"]:
                f(e)

        @block.scalar
        def _(e):
            for f in S.eng["act"]["ops"]:
                f(e)

        @block.vector
        def _(e):
            for f in S.eng["dve"]["ops"]:
                f(e)

        @block.gpsimd
        def _(e):
            for f in S.eng["pool"]["ops"]:
                f(e)

        @block.sync
        def _(e):
            for f in S.eng["sp"]["ops"]:
                f(e)
    return nc


_CACHE = {}


def kernel(x_prompt, x_sample, cache_k, cache_v, state_s, page_table, norm_w, w_in, gain_a, gain_b, sb_bias,
           lb_logits, w_out, final_norm_w):
    f32 = np.float32
    x_prompt = np.asarray(x_prompt, f32)
    x_sample = np.asarray(x_sample, f32)
    cache_k = np.asarray(cache_k, f32)
    cache_v = np.asarray(cache_v, f32)
    state_s = np.asarray(state_s, f32)
    page_table = np.asarray(page_table, np.int32)
    norm_w = np.asarray(norm_w, f32)
    w_in = np.asarray(w_in, f32)[0]
    w_out = np.asarray(w_out, f32)[0]
    gain_a = np.asarray(gain_a, f32)[0]
    gain_b = np.asarray(gain_b, f32)[0]
    sb_bias = np.asarray(sb_bias, f32)[0]
    lb_logits = np.asarray(lb_logits, f32)
    final_norm_w = np.asarray(final_norm_w, f32)

    if "nc" not in _CACHE:
        _CACHE["nc"] = build_program()
        _CACHE["consts"] = make_consts()
    nc = _CACHE["nc"]
    consts = _CACHE["consts"]

    WA = 1024
    in_maps = []
    ck_hg = [np.ascontiguousarray(cache_k[0][:, :, 2 * hg:2 * hg + 2, :]).reshape(NPH * 128, 256) for hg in range(4)]
    cv_hg = [np.ascontiguousarray(cache_v[0][:, :, 2 * hg:2 * hg + 2, :]).reshape(NPH * 128, 256) for hg in range(4)]
    rows = []
    for r in range(4):
        for hh in (2 * r, 2 * r + 1):
            rows.append(np.arange(hh * 128, hh * 128 + 128))
        for hh in (2 * r, 2 * r + 1):
            rows.append(WA + np.arange(hh * 128, hh * 128 + 128))
    wout_perm = np.ascontiguousarray(w_out[np.concatenate(rows)])
    fnw_rep = np.ascontiguousarray(np.broadcast_to(final_norm_w[None, :], (128, D)))
    normw_l = np.ascontiguousarray(norm_w[0].reshape(16, 128).T)
    for c in range(8):
        b, hg = c // 4, c % 4
        h0, h1 = 2 * hg, 2 * hg + 1

        def colsA(base, h):
            return np.arange(base * WA + h * 128, base * WA + h * 128 + 128)

        def colsB(base, h):
            return np.arange(4 * WA + base * WA + h * 128, 4 * WA + base * WA + h * 128 + 128)
        cols = np.concatenate([colsA(0, h0), colsA(0, h1), colsA(3, h0), colsA(3, h1),
                               colsB(0, h0), colsB(0, h1), colsB(3, h0), colsB(3, h1),
                               colsA(1, h0), colsA(1, h1), colsA(2, h0), colsA(2, h1),
                               colsB(1, h0), colsB(1, h1), colsB(2, h0), colsB(2, h1)])
        m = {}
        m["xfull"] = np.ascontiguousarray(x_prompt[b])
        m["xres"] = np.ascontiguousarray(
            x_prompt[b].reshape(NTILE, 4, 128, D)[:, hg].reshape(1024, D))
        m["xs4"] = np.ascontiguousarray(x_sample[4 * b:4 * b + 4].reshape(16, D))
        m["xsres"] = np.ascontiguousarray(x_sample[4 * b + hg])
        m["win"] = np.ascontiguousarray(w_in[:, cols])
        m["wout"] = wout_perm
        m["normw"] = normw_l
        m["gains"] = np.ascontiguousarray(np.stack([gain_a[h0 * 128:h0 * 128 + 128], gain_a[h1 * 128:h1 * 128 + 128],
                                                    gain_b[h0 * 128:h0 * 128 + 128], gain_b[h1 * 128:h1 * 128 + 128]],
                                                   axis=1))
        m["sbb"] = np.ascontiguousarray(np.broadcast_to(sb_bias[[h0, h1]][None, :], (128, 2)))
        m["lbl"] = np.ascontiguousarray(np.broadcast_to(lb_logits[None, :, h0 * 128:h0 * 128 + 256], (128, 2, 256)))
        m["fnw"] = fnw_rep
        m["ck"] = ck_hg[hg]
        m["cv"] = cv_hg[hg]
        m["pt"] = np.ascontiguousarray(np.broadcast_to(page_table[None, 4 * b:4 * b + 4, :], (128, 4, NPG)))
        m["st"] = np.ascontiguousarray(state_s[0, 4 * b:4 * b + 4, h0:h0 + 2])
        m["rank"] = np.array([[hg]], np.int32)
        pp_, rc_ = np.meshgrid(np.arange(128), np.arange(16), indexing="ij")
        m["oidx"] = ((((rc_ // 4) * 4 + hg) * 4 + (rc_ % 4)) * 128 + pp_).astype(np.int32)
        m.update(consts)
        in_maps.append(m)

    res = run_bass_kernel_spmd(nc, in_maps, core_ids=list(range(8)))
    R = res.results
    y_prompt = np.zeros((2, SEQ, D), f32)
    y_sample = np.zeros((8, 4, D), f32)
    k_p = np.zeros((1, 2, SEQ, 8, 128), f32)
    v_p = np.zeros((1, 2, SEQ, 8, 128), f32)
    s_p = np.zeros((1, 2, 8, 128, 128), f32)
    k_s = np.zeros((1, 8, 4, 8, 128), f32)
    v_s = np.zeros((1, 8, 4, 8, 128), f32)
    s_s = np.zeros((1, 8, 8, 128, 128), f32)
    for c in range(8):
        b, hg = c // 4, c % 4
        r = R[c]
        y_prompt[b].reshape(NTILE, 4, 128, D)[:, hg] = np.asarray(r["yp"]).reshape(NTILE, 128, D)
        y_sample[4 * b + hg] = np.asarray(r["ys"])
        k_p[0, b, :, 2 * hg:2 * hg + 2, :] = np.asarray(r["kn"]).reshape(SEQ, 2, 128)
        v_p[0, b, :, 2 * hg:2 * hg + 2, :] = np.asarray(r["vn"]).reshape(SEQ, 2, 128)
        s_p[0, b, 2 * hg:2 * hg + 2] = np.asarray(r["spn"])
        k_s[0, 4 * b:4 * b + 4, :, 2 * hg:2 * hg + 2, :] = np.asarray(r["ksn"]).reshape(4, 4, 2, 128)
        v_s[0, 4 * b:4 * b + 4, :, 2 * hg:2 * hg + 2, :] = np.asarray(r["vsn"]).reshape(4, 4, 2, 128)
        s_s[0, 4 * b:4 * b + 4, 2 * hg:2 * hg + 2] = np.asarray(r["ssn"])
    return (y_prompt, y_sample, k_p, v_p, s_p, k_s, v_s, s_s)
```
